# Optimizing a Trainium2 kernel written in Bass

```python
import jax, jax.numpy as jnp
from jax import lax
import numpy as np

D_MODEL = 4096
BATCH = 2
SEQ = 4096
DEPTH = 2

FOX_HEADS = 16
FOX_HEAD_DIM = 128
FOX_WIDTH = FOX_HEADS * FOX_HEAD_DIM
FOX_BLOCK = 128
GLA_HEADS = 4
GLA_KEY_DIM = 256
GLA_VAL_DIM = 512
GLA_KEY_WIDTH = GLA_HEADS * GLA_KEY_DIM
GLA_WIDTH = GLA_HEADS * GLA_VAL_DIM
GLA_GATE_RANK = 16
GLA_GATE_NORMALIZER = 16.0
GLA_CHUNK = 64
MIX_WIDTH = FOX_WIDTH + GLA_WIDTH
IN_SIZES = (FOX_WIDTH, FOX_WIDTH, FOX_WIDTH, FOX_HEADS,
            GLA_KEY_WIDTH, GLA_KEY_WIDTH, GLA_WIDTH, GLA_WIDTH, GLA_GATE_RANK)
VALUE_SLOTS = (2, 6)
IN_COLS = sum(IN_SIZES)
D_FF = 256 * ((8 * D_MODEL + 3 * 256 - 1) // (3 * 256))
ALPHA = (2 * DEPTH) ** 0.25
BETA = (8 * DEPTH) ** -0.25
EPS = 1e-5

kernel_name = "fox_gla_parallel_heads_deepnorm"


def layer_norm(t, g, b):
    t32 = t.astype(jnp.float32)
    mu = jnp.mean(t32, axis=-1, keepdims=True)
    var = jnp.mean(jnp.square(t32 - mu), axis=-1, keepdims=True)
    return ((t32 - mu) * lax.rsqrt(var + EPS) * g + b).astype(t.dtype)


def rms_norm(t, g):
    t32 = t.astype(jnp.float32)
    return t32 * lax.rsqrt(jnp.mean(jnp.square(t32), axis=-1, keepdims=True) + EPS) * g


def fox_attention(q, k, v, c):
    B, H, S, D = q.shape
    n_blocks = S // FOX_BLOCK
    scale = D ** -0.5
    k_pos = jnp.arange(S)

    def block(i):
        start = i * FOX_BLOCK
        qb = lax.dynamic_slice_in_dim(q, start, FOX_BLOCK, axis=2)
        cb = lax.dynamic_slice_in_dim(c, start, FOX_BLOCK, axis=2)
        logits = jnp.einsum('bhqd,bhkd->bhqk', qb, k) * scale + cb[..., :, None] - c[..., None, :]
        q_pos = start + jnp.arange(FOX_BLOCK)
        logits = jnp.where(q_pos[:, None] >= k_pos[None, :], logits, -jnp.inf)
        p = jax.nn.softmax(logits, axis=-1)
        return jnp.einsum('bhqk,bhkd->bhqd', p, v)

    out = lax.map(block, jnp.arange(n_blocks))
    return out.transpose(1, 2, 0, 3, 4).reshape(B, H, S, D)


def gla_chunked(q, k, v, g):
    B, S, H, DK = q.shape
    DV = v.shape[-1]
    n_chunks = S // GLA_CHUNK

    def to_chunks(t):
        return t.reshape(B, n_chunks, GLA_CHUNK, H, t.shape[-1]).transpose(1, 0, 3, 2, 4)

    qc, kc, vc, gc = to_chunks(q), to_chunks(k), to_chunks(v), to_chunks(g)
    bc = jnp.cumsum(gc, axis=-2)
    causal = jnp.tril(jnp.ones((GLA_CHUNK, GLA_CHUNK), dtype=bool))

    def step(state, inp):
        q_, k_, v_, b_ = inp
        b_last = b_[..., -1, :]
        o_inter = jnp.einsum('bhid,bhde->bhie', q_ * jnp.exp(b_), state)
        rel = b_[:, :, :, None, :] - b_[:, :, None, :, :]
        decay = jnp.exp(jnp.where(causal[None, None, :, :, None], rel, -jnp.inf))
        scores = jnp.einsum('bhid,bhjd,bhijd->bhij', q_, k_, decay)
        o_intra = jnp.einsum('bhij,bhje->bhie', scores, v_)
        k_dec = k_ * jnp.exp(b_last[:, :, None, :] - b_)
        state = state * jnp.exp(b_last)[..., None] + jnp.einsum('bhjd,bhje->bhde', k_dec, v_)
        return state, o_inter + o_intra

    state0 = jnp.zeros((B, H, DK, DV), jnp.float32)
    _, out = lax.scan(step, state0, (qc, kc, vc, bc))
    return out.transpose(1, 0, 3, 2, 4).reshape(B, S, H, DV)


def hybrid_mixer(h, w_in, b_f, w_gk_up, b_gk, fox_norm_g, gla_norm_g, w_out):
    B, S, _ = h.shape
    f32 = jnp.float32
    split_at = []
    acc = 0
    for size in IN_SIZES[:-1]:
        acc += size
        split_at.append(acc)
    proj = h @ w_in
    fq, fk, fv, f_logit, gq, gk, gv, gg, g_low = jnp.split(proj, split_at, axis=-1)

    def heads(t):
        return t.reshape(B, S, FOX_HEADS, FOX_HEAD_DIM).transpose(0, 2, 1, 3).astype(f32)
    log_f = jax.nn.log_sigmoid((f_logit + b_f).astype(f32))
    c = jnp.cumsum(log_f, axis=1).transpose(0, 2, 1)
    o_fox = fox_attention(heads(fq), heads(fk), heads(fv), c)
    o_fox = rms_norm(o_fox, fox_norm_g[:, None, :])
    o_fox = o_fox.transpose(0, 2, 1, 3).reshape(B, S, FOX_WIDTH)

    log_a = jax.nn.log_sigmoid((g_low @ w_gk_up + b_gk).astype(f32)) / GLA_GATE_NORMALIZER
    q_g = gq.reshape(B, S, GLA_HEADS, GLA_KEY_DIM).astype(f32) * (GLA_KEY_DIM ** -0.5)
    k_g = gk.reshape(B, S, GLA_HEADS, GLA_KEY_DIM).astype(f32)
    v_g = gv.reshape(B, S, GLA_HEADS, GLA_VAL_DIM).astype(f32)
    a_g = log_a.reshape(B, S, GLA_HEADS, GLA_KEY_DIM)
    o_gla = gla_chunked(q_g, k_g, v_g, a_g)
    gate = jax.nn.silu(gg.reshape(B, S, GLA_HEADS, GLA_VAL_DIM).astype(f32))
    o_gla = (rms_norm(o_gla, gla_norm_g) * gate).reshape(B, S, GLA_WIDTH)

    mixed = jnp.concatenate([o_fox, o_gla], axis=-1).astype(h.dtype)
    return mixed @ w_out


def swiglu(h, w_gate, w_up, w_down):
    return (jax.nn.silu(h @ w_gate) * (h @ w_up)) @ w_down


def setup_inputs(seed: int = 0) -> dict:
    key = jax.random.key(seed)
    ks = jax.random.split(key, 17)
    f32 = jnp.float32

    def nrm(k, shape, scale):
        return jax.random.normal(k, shape, f32) * scale

    col_scale = jnp.concatenate([jnp.full((s,), BETA if i in VALUE_SLOTS else 1.0, f32)
                                 for i, s in enumerate(IN_SIZES)])
    return {
        "x": nrm(ks[0], (BATCH, SEQ, D_MODEL), 1.0),
        "ln_in_g": 1.0 + nrm(ks[1], (D_MODEL,), 0.01),
        "ln_in_b": nrm(ks[2], (D_MODEL,), 0.01),
        "w_in": nrm(ks[3], (DEPTH, D_MODEL, IN_COLS), D_MODEL ** -0.5) * col_scale,
        "b_f": jax.random.uniform(ks[4], (DEPTH, FOX_HEADS), f32, 3.0, 6.0),
        "w_gk_up": nrm(ks[5], (DEPTH, GLA_GATE_RANK, GLA_KEY_WIDTH), GLA_GATE_RANK ** -0.5),
        "b_gk": nrm(ks[6], (DEPTH, GLA_KEY_WIDTH), 0.01),
        "fox_norm_g": 1.0 + nrm(ks[7], (DEPTH, FOX_HEADS, FOX_HEAD_DIM), 0.01),
        "gla_norm_g": 1.0 + nrm(ks[8], (DEPTH, GLA_VAL_DIM), 0.01),
        "w_out": nrm(ks[9], (DEPTH, MIX_WIDTH, D_MODEL), BETA * MIX_WIDTH ** -0.5),
        "ln1_g": 1.0 + nrm(ks[10], (DEPTH, D_MODEL), 0.01),
        "ln1_b": nrm(ks[11], (DEPTH, D_MODEL), 0.01),
        "w_gate": nrm(ks[12], (DEPTH, D_MODEL, D_FF), BETA * D_MODEL ** -0.5),
        "w_up": nrm(ks[13], (DEPTH, D_MODEL, D_FF), BETA * D_MODEL ** -0.5),
        "w_down": nrm(ks[14], (DEPTH, D_FF, D_MODEL), BETA * D_FF ** -0.5),
        "ln2_g": 1.0 + nrm(ks[15], (DEPTH, D_MODEL), 0.01),
        "ln2_b": nrm(ks[16], (DEPTH, D_MODEL), 0.01),
    }


def reference(x, ln_in_g, ln_in_b, w_in, b_f, w_gk_up, b_gk, fox_norm_g, gla_norm_g,
              w_out, ln1_g, ln1_b, w_gate, w_up, w_down, ln2_g, ln2_b):
    x = layer_norm(x, ln_in_g, ln_in_b)
    for l in range(DEPTH):
        mix = hybrid_mixer(x, w_in[l], b_f[l], w_gk_up[l], b_gk[l],
                           fox_norm_g[l], gla_norm_g[l], w_out[l])
        x = layer_norm(ALPHA * x + mix, ln1_g[l], ln1_b[l])
        ffn = swiglu(x, w_gate[l], w_up[l], w_down[l])
        x = layer_norm(ALPHA * x + ffn, ln2_g[l], ln2_b[l])
    return x
```

```python
from contextlib import ExitStack
import numpy as np
import concourse.bass as bass
import concourse.mybir as mybir
from concourse.bass_utils import run_bass_kernel_spmd

F32 = mybir.dt.float32
BF16 = mybir.dt.bfloat16
AF = mybir.ActivationFunctionType
ALU = mybir.AluOpType

S = 4096
D = 4096
DFF = 11008
NFF = DFF // 128
DEPTH = 2
ALPHA = (2 * DEPTH) ** 0.25
EPS = 1e-5
ENGS = ("pe", "act", "dve", "pool", "sp")
FQ0, FK0, FV0, GQ0, GK0, GV0, GG0 = 0, 2048, 4096, 6144, 7168, 8192, 10240


class Buf:
    __slots__ = ("t", "w", "r", "sem")

    def __init__(self, t):
        self.t = t
        self.w = None
        self.r = []
        self.sem = None


class Prog:
    def __init__(self, nc):
        self.nc = nc
        self.es = ExitStack()
        self.q = {e: [] for e in ENGS}
        self.cnt = {e: 0 for e in ENGS}
        self.esem = {e: self.es.enter_context(nc.semaphore("es_" + e)) for e in ENGS}
        self.dsems = []
        self.waited = {e: {} for e in ENGS}
        self.n = 0
        self.allbufs = []
        self.free_sems = []

    def _nm(self, p):
        self.n += 1
        return f"{p}{self.n}"

    def sb(self, shape, dt, stack=None):
        b = Buf((stack or self.es).enter_context(self.nc.sbuf_tensor(self._nm("sb"), list(shape), dt)))
        self.allbufs.append(b)
        return b

    def ps(self, shape, dt, stack=None):
        b = Buf((stack or self.es).enter_context(self.nc.psum_tensor(self._nm("ps"), list(shape), dt)))
        self.allbufs.append(b)
        return b

    def dram(self, name, shape, dt, kind="Internal"):
        b = Buf(self.nc.dram_tensor(name, list(shape), dt, kind=kind).ap())
        self.allbufs.append(b)
        return b

    def sub(self, buf):
        b = Buf(buf.t)
        self.allbufs.append(b)
        return b

    def _wait(self, e, tok):
        if tok is None:
            return
        kind, key, val = tok
        if kind == "e":
            if e == "pe" and key == "pe":
                return
            sem = self.esem[key]
        else:
            sem = self.dsems[key][0]
            val = self.dsems[key][1]
        k = (kind, key)
        if self.waited[e].get(k, 0) >= val:
            return
        self.waited[e][k] = val
        self.q[e].append(("w", sem, val))

    def _deps(self, e, reads, writes, extra):
        for b in reads:
            self._wait(e, b.w)
        for b in writes:
            self._wait(e, b.w)
            for t in b.r:
                self._wait(e, t)
        for t in extra:
            self._wait(e, t)

    def _commit(self, tok, reads, writes):
        for b in reads:
            b.r.append(tok)
            if len(b.r) > 24:
                b.r = b.r[-24:]
        for b in writes:
            b.w = tok
            b.r = []

    def do(self, e, fn, reads=(), writes=(), extra=(), mark=True):
        self._deps(e, reads, writes, extra)
        tok = None
        if mark:
            self.cnt[e] += 1
            tok = ("e", e, self.cnt[e])
        self.q[e].append(("o", fn, self.esem[e] if mark else None, 1))
        if tok:
            self._commit(tok, reads, writes)
        return tok

    def mm(self, mms, reads, writes):
        self._deps("pe", reads, writes, ())
        for fn in mms[:-1]:
            self.q["pe"].append(("o", fn, None, 1))
        self.cnt["pe"] += 1
        tok = ("e", "pe", self.cnt["pe"])
        self.q["pe"].append(("o", mms[-1], self.esem["pe"], 1))
        self._commit(tok, reads, writes)
        return tok

    def dma(self, e, fn, reads, writes, sembuf, extra=()):
        self._deps(e, reads, writes, extra)
        if sembuf.sem is None:
            if self.free_sems:
                sembuf.sem = self.free_sems.pop()
            else:
                self.dsems.append([self.es.enter_context(self.nc.semaphore(self._nm("ds"))), 0])
                sembuf.sem = len(self.dsems) - 1
        rec = self.dsems[sembuf.sem]
        rec[1] += 16
        tok = ("d", sembuf.sem, rec[1])
        self.q[e].append(("o", fn, rec[0], 16))
        self._commit(tok, reads, writes)
        return tok

    def barrier(self):
        for e in ENGS:
            for k in ("pe", "act", "dve", "pool"):
                if self.cnt[k] > 0 and not (e == k):
                    self._wait(e, ("e", k, self.cnt[k]))
            for i, rec in enumerate(self.dsems):
                if rec[1] > 0:
                    self._wait(e, ("d", i, rec[1]))
        for b in self.allbufs:
            b.w = None
            b.r = []
            if b.sem is not None:
                self.free_sems.append(b.sem)
                b.sem = None

    def finish(self):
        nc, q = self.nc, self.q

        def run(eng, lst):
            for it in lst:
                if it[0] == "w":
                    eng.wait_ge(it[1], it[2])
                else:
                    ins = it[1](eng)
                    if it[2] is not None:
                        ins.then_inc(it[2], it[3])

        with nc.Block() as block:
            @block.tensor
            def _(eng):
                run(eng, q["pe"])

            @block.scalar
            def _(eng):
                run(eng, q["act"])

            @block.vector
            def _(eng):
                run(eng, q["dve"])

            @block.gpsimd
            def _(eng):
                run(eng, q["pool"])

            @block.sync
            def _(eng):
                run(eng, q["sp"])
        self.es.close()


class RR:
    def __init__(self, bufs):
        self.b = bufs
        self.i = 0

    def get(self):
        b = self.b[self.i % len(self.b)]
        self.i += 1
        return b


def phase_A(P, G, l, x_src, apply_ln, stage_limit=None):
    TGA = 1024
    with ExitStack() as st:
        xnT = P.sb([128, 32, TGA], BF16, st)
        xts = RR([P.sb([128, D], F32, st) for _ in range(2)])
        xbs = RR([P.sb([128, D], BF16, st) for _ in range(2)])
        wp = RR([P.sb([128, 32, 128], BF16, st) for _ in range(4)])
        wsm = P.sb([128, 32, 32], BF16, st)
        stg = RR([P.sb([128, 512], BF16, st) for _ in range(4)])
        stgf = RR([P.sb([32, 512], F32, st) for _ in range(2)])
        stat = RR([P.sb([128, 8], F32, st) for _ in range(2)])
        junk = P.sb([128, D], BF16, st)
        pt = RR([P.ps([128, 1024], BF16, st) for _ in range(2)])
        pm = RR([P.ps([128, 512], F32, st) for _ in range(4)])
        if apply_ln:
            gbc = P.sb([128, D], F32, st)
            bbc = P.sb([128, D], F32, st)
            P.dma("sp", lambda e: e.dma_start(out=gbc.t[:], in_=G.ln_in_g.t.partition_broadcast(128)), [], [gbc], gbc)
            P.dma("sp", lambda e: e.dma_start(out=bbc.t[:], in_=G.ln_in_b.t.partition_broadcast(128)), [], [bbc], bbc)
        P.dma("pool", lambda e: e.dma_start(out=wsm.t[:], in_=G.wsm[l].t), [], [wsm], wsm)
        ev = 0
        for tg in range(S // TGA):
            if stage_limit is not None and tg >= stage_limit:
                break
            for tt in range(TGA // 128):
                r0 = tg * TGA + tt * 128
                xt = xts.get()
                xb = xbs.get()
                P.dma("sp", lambda e, xt=xt, r0=r0: e.dma_start(out=xt.t[:], in_=x_src.t[r0:r0 + 128, :]), [], [xt], xt)
                if apply_ln:
                    sc = stat.get()
                    P.do("dve", lambda e, sc=sc: e.memset(sc.t[:], 0.0), [], [sc])
                    P.do("act", lambda e, xt=xt, sc=sc: e.activation(out=junk.t[:], in_=xt.t[:], func=AF.Identity,
                                                                   accum_out=sc.t[:, 0:1]), [xt], [junk, sc])
                    P.do("act", lambda e, xt=xt, sc=sc: e.activation(out=junk.t[:], in_=xt.t[:], func=AF.Square,
                                                                   accum_out=sc.t[:, 1:2]), [xt], [junk, sc])
                    P.do("dve", lambda e, sc=sc: e.tensor_scalar(out=sc.t[:, 2:4], in0=sc.t[:, 0:2], scalar1=1.0 / D,
                                                               scalar2=None, op0=ALU.mult), [sc], [sc])
                    P.do("dve", lambda e, sc=sc: e.tensor_tensor(out=sc.t[:, 4:5], in0=sc.t[:, 2:3], in1=sc.t[:, 2:3],
                                                               op=ALU.mult), [sc], [sc])
                    P.do("dve", lambda e, sc=sc: e.tensor_tensor(out=sc.t[:, 5:6], in0=sc.t[:, 3:4], in1=sc.t[:, 4:5],
                                                               op=ALU.subtract), [sc], [sc])
                    P.do("act", lambda e, sc=sc: e.activation(out=sc.t[:, 6:7], in_=sc.t[:, 5:6], func=AF.Ln, bias=EPS),
                         [sc], [sc])
                    P.do("act", lambda e, sc=sc: e.activation(out=sc.t[:, 7:8], in_=sc.t[:, 6:7], func=AF.Exp, scale=-0.5),
                         [sc], [sc])
                    P.do("dve", lambda e, xt=xt, sc=sc: e.tensor_scalar(out=xt.t[:], in0=xt.t[:], scalar1=sc.t[:, 2:3],
                                                                      scalar2=sc.t[:, 7:8], op0=ALU.subtract, op1=ALU.mult),
                         [xt, sc], [xt])
                    P.do("pool", lambda e, xt=xt: e.tensor_tensor(out=xt.t[:], in0=xt.t[:], in1=gbc.t[:], op=ALU.mult),
                         [xt, gbc], [xt])
                    P.do("pool", lambda e, xt=xt: e.tensor_tensor(out=xt.t[:], in0=xt.t[:], in1=bbc.t[:], op=ALU.add),
                         [xt, bbc], [xt])
                    P.dma("sp", lambda e, xt=xt, r0=r0: e.dma_start(out=G.xn.t[r0:r0 + 128, :], in_=xt.t[:]), [xt], [], xt)
                P.do("act", lambda e, xt=xt, xb=xb: e.copy(out=xb.t[:], in_=xt.t[:]), [xt], [xb])
                for k8 in range(4):
                    p = pt.get()
                    P.mm([(lambda e, p=p, xb=xb, k8=k8, j=j: e.transpose(p.t[:, j * 128:(j + 1) * 128],
                                                                       xb.t[:, (k8 * 8 + j) * 128:(k8 * 8 + j + 1) * 128],
                                                                       G.ident.t[:])) for j in range(8)], [xb, G.ident], [p])
                    eng = "dve" if (ev % 2 == 0) else "act"
                    ev += 1
                    src = p.t[:].rearrange("p (j t) -> p j t", t=128)
                    dst = xnT.t[:, k8 * 8:(k8 + 1) * 8, tt * 128:(tt + 1) * 128]
                    if eng == "dve":
                        P.do("dve", lambda e, src=src, dst=dst: e.tensor_copy(out=dst, in_=src), [p], [xnT])
                    else:
                        P.do("act", lambda e, src=src, dst=dst: e.copy(out=dst, in_=src), [p], [xnT])
            for mt in range(97):
                small = (mt == 96)
                if small:
                    w = wsm
                    M = 32
                else:
                    w = wp.get()
                    M = 128
                    P.dma("pool", lambda e, w=w, mt=mt: e.dma_start(out=w.t[:], in_=G.win[l].t[mt]), [], [w], w)
                for hf in range(TGA // 512):
                    acc = pm.get()
                    P.mm([(lambda e, acc=acc, w=w, kc=kc, hf=hf, M=M: e.matmul(acc.t[0:M, :], w.t[:, kc, 0:M],
                                                                             xnT.t[:, kc, hf * 512:(hf + 1) * 512],
                                                                             start=(kc == 0), stop=(kc == 31)))
                          for kc in range(32)], [w, xnT], [acc])
                    c0 = tg * TGA + hf * 512
                    eng = "dve" if (ev % 2 == 0) else "act"
                    ev += 1
                    if small:
                        sg = stgf.get()
                        P.do("dve", lambda e, sg=sg, acc=acc: e.tensor_copy(out=sg.t[:], in_=acc.t[0:32, :]), [acc], [sg])
                        P.dma("sp", lambda e, sg=sg, c0=c0: e.dma_start(out=G.projF.t[:, c0:c0 + 512], in_=sg.t[:]),
                              [sg], [], sg)
                    else:
                        sg = stg.get()
                        if eng == "dve":
                            P.do("dve", lambda e, sg=sg, acc=acc: e.tensor_copy(out=sg.t[:], in_=acc.t[:]), [acc], [sg])
                        else:
                            P.do("act", lambda e, sg=sg, acc=acc: e.copy(out=sg.t[:], in_=acc.t[:]), [acc], [sg])
                        P.dma("sp", lambda e, sg=sg, c0=c0, mt=mt: e.dma_start(
                            out=G.projT.t[mt * 128:(mt + 1) * 128, c0:c0 + 512], in_=sg.t[:]), [sg], [], sg)
        P.barrier()


def phase_B_fox(P, G, l, heads=range(16)):
    SCALE = 128 ** -0.5
    with ExitStack() as st:
        fl = P.sb([16, S], F32, st)
        csp = P.sb([16, S], F32, st)
        nbf = P.sb([16, 1], F32, st)
        ones16 = P.sb([16, 1], F32, st)
        id16 = P.sb([16, 16], F32, st)
        sel = P.sb([16, 16, 128], F32, st)
        cspcol = P.sb([128, 32, 16], F32, st)
        crefbc = P.sb([128, 16, 8], F32, st)
        gfox = P.sb([128, 16], F32, st)
        ones_bf = P.sb([128, 128], BF16, st)
        ones_f = P.sb([128, 128], F32, st)
        qTs = RR([P.sb([128, S], BF16, st) for _ in range(2)])
        kTs = RR([P.sb([128, S], BF16, st) for _ in range(2)])
        vTs = RR([P.sb([128, S], BF16, st) for _ in range(2)])
        vtoks = RR([P.sb([128, 32, 128], BF16, st) for _ in range(2)])
        nbs = RR([P.sb([128, 32], F32, st) for _ in range(2)])
        pts = RR([P.sb([128, 512], BF16, st) for _ in range(4)])
        rls = RR([P.sb([128, 512], F32, st) for _ in range(2)])
        os_ = RR([P.sb([128, 512], F32, st) for _ in range(2)])
        sqs = RR([P.sb([128, 512], F32, st) for _ in range(2)])
        rstds = RR([P.sb([128, 512], F32, st) for _ in range(2)])
        outs = RR([P.sb([128, 512], BF16, st) for _ in range(2)])
        ps_s = RR([P.ps([128, 512], F32, st) for _ in range(2)])
        ps_o = RR([P.ps([128, 512], F32, st) for _ in range(2)])
        ps_l = RR([P.ps([128, 512], F32, st) for _ in range(2)])
        ps_x = P.ps([128, 512], F32, st)
        ps_t = P.ps([128, 1024], BF16, st)

        P.dma("sp", lambda e: e.dma_start(out=fl.t[:], in_=G.projF.t[0:16, :]), [], [fl], fl)
        P.dma("sp", lambda e: e.dma_start(out=nbf.t[:], in_=G.b_f[l].t), [], [nbf], nbf)
        P.dma("sp", lambda e: e.dma_start(out=gfox.t[:], in_=G.gfox[l].t), [], [gfox], gfox)
        P.do("dve", lambda e: e.tensor_scalar(out=nbf.t[:], in0=nbf.t[:], scalar1=-1.0, scalar2=None, op0=ALU.mult), [nbf], [nbf])
        P.do("pool", lambda e: e.memset(ones16.t[:], 1.0), [], [ones16])
        P.do("pool", lambda e: e.memset(ones_bf.t[:], 1.0), [], [ones_bf])
        P.do("pool", lambda e: e.memset(ones_f.t[:], 1.0), [], [ones_f])
        P.do("pool", lambda e: e.memset(id16.t[:], 1.0), [], [id16])
        P.do("pool", lambda e: e.affine_select(out=id16.t[:], in_=id16.t[:], pattern=[[-1, 16]], compare_op=ALU.is_equal,
                                               fill=0.0, base=0, channel_multiplier=1), [id16], [id16])
        P.do("pool", lambda e: e.memset(sel.t[:], 1.0), [], [sel])
        for h in range(16):
            P.do("pool", lambda e, h=h: e.affine_select(out=sel.t[:, h, :], in_=sel.t[:, h, :], pattern=[[0, 128]],
                                                        compare_op=ALU.is_equal, fill=0.0, base=-h, channel_multiplier=1),
                 [sel], [sel])
        P.do("act", lambda e: e.activation(out=fl.t[:], in_=fl.t[:], func=AF.Exp, bias=nbf.t[:], scale=-1.0), [fl, nbf], [fl])
        P.do("act", lambda e: e.activation(out=fl.t[:], in_=fl.t[:], func=AF.Ln, bias=1.0), [fl], [fl])
        P.do("dve", lambda e: e.tensor_tensor_scan(out=csp.t[:], data0=ones16.t[:, 0:1].broadcast_to([16, S]), data1=fl.t[:],
                                                   initial=0.0, op0=ALU.mult, op1=ALU.add), [fl, ones16], [csp])
        px = ps_x.t[:].rearrange("p (b h) -> p b h", h=16)
        P.mm([(lambda e, b=b: e.matmul(px[:, b, :], csp.t[:, b * 128:(b + 1) * 128], id16.t[:], start=True, stop=True))
              for b in range(32)], [csp, id16], [ps_x])
        P.do("dve", lambda e: e.tensor_copy(out=cspcol.t[:], in_=px), [ps_x], [cspcol])
        px2 = ps_x.t[:, 0:128].rearrange("p (h q) -> p h q", q=8)
        cq = csp.t[:].rearrange("p (q t) -> p q t", t=512)[:, :, 0:1].rearrange("p q o -> p (q o)")
        P.mm([(lambda e, h=h: e.matmul(px2[:, h, :], sel.t[:, h, :], cq, start=True, stop=True)) for h in range(16)],
             [csp, sel], [ps_x])
        P.do("dve", lambda e: e.tensor_copy(out=crefbc.t[:], in_=px2), [ps_x], [crefbc])

        for h in heads:
            qT, kT, vT, vtok = qTs.get(), kTs.get(), vTs.get(), vtoks.get()
            P.dma("sp", lambda e, qT=qT, h=h: e.dma_start(out=qT.t[:], in_=G.projT.t[FQ0 + h * 128:FQ0 + (h + 1) * 128, :]),
                  [], [qT], qT)
            P.dma("sp", lambda e, kT=kT, h=h: e.dma_start(out=kT.t[:], in_=G.projT.t[FK0 + h * 128:FK0 + (h + 1) * 128, :]),
                  [], [kT], kT)
            P.dma("sp", lambda e, vT=vT, h=h: e.dma_start(out=vT.t[:], in_=G.projT.t[FV0 + h * 128:FV0 + (h + 1) * 128, :]),
                  [], [vT], vT)
            for b8 in range(4):
                P.mm([(lambda e, b8=b8, j=j, vT=vT: e.transpose(ps_t.t[:, j * 128:(j + 1) * 128],
                                                               vT.t[:, (b8 * 8 + j) * 128:(b8 * 8 + j + 1) * 128],
                                                               G.ident.t[:])) for j in range(8)], [vT, G.ident], [ps_t])
                P.do("dve", lambda e, b8=b8, vtok=vtok: e.tensor_copy(
                    out=vtok.t[:, b8 * 8:(b8 + 1) * 8, :], in_=ps_t.t[:].rearrange("p (j t) -> p j t", t=128)), [ps_t], [vtok])
            for qb in range(8):
                q0 = qb * 512
                nkb = 4 * qb + 4
                nb = nbs.get()
                P.do("dve", lambda e, nb=nb, nkb=nkb, h=h, qb=qb: e.tensor_scalar(
                    out=nb.t[:, 0:nkb], in0=cspcol.t[:, 0:nkb, h], scalar1=crefbc.t[:, h, qb:qb + 1], scalar2=None,
                    op0=ALU.subtract), [cspcol, crefbc], [nb])
                po, pl = ps_o.get(), ps_l.get()
                order = [4 * qb + j for j in range(4)] + list(range(4 * qb))
                for i, kb in enumerate(order):
                    j = kb - 4 * qb
                    off = 128 * j if j > 0 else 0
                    nco = 512 - off
                    pss = ps_s.get()
                    pt = pts.get()
                    first, last = (i == 0), (i == len(order) - 1)
                    P.mm([lambda e, pss=pss, kT=kT, qT=qT, kb=kb, q0=q0, off=off, nco=nco: e.matmul(
                        pss.t[:, 0:nco], kT.t[:, kb * 128:(kb + 1) * 128], qT.t[:, q0 + off:q0 + 512], start=True, stop=True)],
                        [kT, qT], [pss])
                    P.do("act", lambda e, pt=pt, pss=pss, nb=nb, kb=kb, nco=nco: e.activation(
                        out=pt.t[:, 0:nco], in_=pss.t[:, 0:nco], func=AF.Exp, bias=nb.t[:, kb:kb + 1], scale=SCALE),
                        [pss, nb], [pt])
                    if j >= 0:
                        P.do("pool", lambda e, pt=pt: e.affine_select(out=pt.t[:, 0:128], in_=pt.t[:, 0:128], pattern=[[1, 128]],
                                                                    compare_op=ALU.is_ge, fill=0.0, base=0,
                                                                    channel_multiplier=-1), [pt], [pt])
                    P.mm([lambda e, po=po, vtok=vtok, pt=pt, kb=kb, off=off, nco=nco, first=first, last=last: e.matmul(
                        po.t[:, off:512], vtok.t[:, kb, :], pt.t[:, 0:nco], start=first, stop=last),
                        lambda e, pl=pl, pt=pt, off=off, nco=nco, first=first, last=last: e.matmul(
                        pl.t[:, off:512], ones_bf.t[:], pt.t[:, 0:nco], start=first, stop=last)],
                        [vtok, pt, ones_bf], [po, pl])
                rl, o, sq, rstd, ob = rls.get(), os_.get(), sqs.get(), rstds.get(), outs.get()
                P.do("dve", lambda e, rl=rl, pl=pl: e.reciprocal(out=rl.t[:], in_=pl.t[:]), [pl], [rl])
                P.do("dve", lambda e, o=o, po=po, rl=rl: e.tensor_tensor(out=o.t[:], in0=po.t[:], in1=rl.t[:], op=ALU.mult),
                     [po, rl], [o])
                P.do("act", lambda e, sq=sq, o=o: e.activation(out=sq.t[:], in_=o.t[:], func=AF.Square), [o], [sq])
                P.mm([lambda e, sq=sq: e.matmul(ps_x.t[:], ones_f.t[:], sq.t[:], start=True, stop=True)], [sq, ones_f], [ps_x])
                P.do("act", lambda e, rstd=rstd: e.activation(out=rstd.t[:], in_=ps_x.t[:], func=AF.Ln, bias=EPS, scale=1.0 / 128),
                     [ps_x], [rstd])
                P.do("act", lambda e, rstd=rstd: e.activation(out=rstd.t[:], in_=rstd.t[:], func=AF.Exp, scale=-0.5),
                     [rstd], [rstd])
                P.do("dve", lambda e, ob=ob, o=o, rstd=rstd, h=h: e.scalar_tensor_tensor(
                    out=ob.t[:], in0=o.t[:], scalar=gfox.t[:, h:h + 1], in1=rstd.t[:], op0=ALU.mult, op1=ALU.mult),
                    [o, rstd, gfox], [ob])
                P.dma("sp", lambda e, ob=ob, h=h, q0=q0: e.dma_start(out=G.mixT.t[h * 128:(h + 1) * 128, q0:q0 + 512], in_=ob.t[:]),
                      [ob], [], ob)
        P.barrier()


def phase_B_gla(P, G, l, heads=range(4)):
    with ExitStack() as st:
        wup = P.sb([16, 1024], F32, st)
        nbgk = P.sb([128, 8], F32, st)
        ggla = P.sb([128, 4], F32, st)
        rmask = P.sb([128, 512], F32, st)
        tri = P.sb([128, 128], F32, st)
        ones_f = P.sb([128, 128], F32, st)
        Sst = P.sb([128, 2, 512], F32, st)
        Sbf = P.sb([128, 2, 512], BF16, st)
        gqs = RR([P.sb([128, 2, 512], BF16, st) for _ in range(2)])
        gks = RR([P.sb([128, 2, 512], BF16, st) for _ in range(2)])
        gvs = RR([P.sb([128, 4, 512], BF16, st) for _ in range(2)])
        ggs = RR([P.sb([128, 4, 512], BF16, st) for _ in range(2)])
        glows = RR([P.sb([16, 512], F32, st) for _ in range(2)])
        spb = P.sb([128, 2, 512], F32, st)
        bsp = P.sb([128, 2, 512], F32, st)
        dd = P.sb([128, 2, 512], F32, st)
        eb = P.sb([128, 2, 512], F32, st)
        enb = P.sb([128, 2, 512], F32, st)
        ekd = P.sb([128, 2, 512], F32, st)
        qe = P.sb([128, 2, 512], BF16, st)
        ke = P.sb([128, 2, 512], BF16, st)
        kdT = P.sb([128, 2, 512], BF16, st)
        kdtok = P.sb([128, 4, 256], BF16, st)
        vtok = P.sb([128, 4, 512], BF16, st)
        scms = RR([P.sb([128, 128], BF16, st) for _ in range(2)])
        og = P.sb([128, 4, 512], F32, st)
        sq = P.sb([128, 4, 512], F32, st)
        rstd = P.sb([128, 512], F32, st)
        sig = P.sb([128, 4, 512], F32, st)
        t1 = P.sb([128, 4, 512], F32, st)
        ob = P.sb([128, 4, 512], BF16, st)
        ps_z = [P.ps([128, 512], F32, st) for _ in range(2)]
        ps_t = P.ps([128, 2, 1024], BF16, st)
        ps_sc = P.ps([128, 128], F32, st)
        ps_o = P.ps([128, 4, 128], F32, st)
        ps_su = [P.ps([128, 512], F32, st) for _ in range(2)]

        P.dma("sp", lambda e: e.dma_start(out=wup.t[:], in_=G.w_gk_up[l].t), [], [wup], wup)
        P.dma("sp", lambda e: e.dma_start(out=nbgk.t[:], in_=G.bgk[l].t), [], [nbgk], nbgk)
        P.dma("sp", lambda e: e.dma_start(out=ggla.t[:], in_=G.ggla[l].t), [], [ggla], ggla)
        P.do("dve", lambda e: e.tensor_scalar(out=nbgk.t[:], in0=nbgk.t[:], scalar1=-1.0, scalar2=None, op0=ALU.mult), [nbgk], [nbgk])
        P.do("pool", lambda e: e.memset(rmask.t[:], 1.0), [], [rmask])
        P.do("pool", lambda e: e.memset(rmask.t[:].rearrange("p (c t) -> p c t", t=128)[:, :, 0:1], 0.0), [rmask], [rmask])
        P.do("pool", lambda e: e.memset(tri.t[:], 1.0), [], [tri])
        P.do("pool", lambda e: e.affine_select(out=tri.t[:], in_=tri.t[:], pattern=[[1, 128]], compare_op=ALU.is_ge, fill=0.0,
                                               base=0, channel_multiplier=-1), [tri], [tri])
        P.do("pool", lambda e: e.memset(ones_f.t[:], 1.0), [], [ones_f])

        for g in heads:
            P.do("dve", lambda e: e.memset(Sst.t[:], 0.0), [], [Sst])
            P.do("pool", lambda e: e.memset(Sbf.t[:], 0.0), [], [Sbf])
            for tg in range(8):
                c0 = tg * 512
                gq, gk, gv, gg, glow = gqs.get(), gks.get(), gvs.get(), ggs.get(), glows.get()

                def ld(buf, row0, nch, c0=c0):
                    src = G.projT.t[row0:row0 + nch * 128, c0:c0 + 512].rearrange("(c p) t -> p c t", p=128)
                    P.dma("sp", lambda e, buf=buf, src=src: e.dma_start(out=buf.t[:], in_=src), [], [buf], buf)
                ld(gq, GQ0 + g * 256, 2)
                ld(gk, GK0 + g * 256, 2)
                ld(gv, GV0 + g * 512, 4)
                ld(gg, GG0 + g * 512, 4)
                P.dma("sp", lambda e, glow=glow, c0=c0: e.dma_start(out=glow.t[:], in_=G.projF.t[16:32, c0:c0 + 512]),
                      [], [glow], glow)
                for dh in range(2):
                    P.mm([lambda e, dh=dh, glow=glow, g=g: e.matmul(ps_z[dh].t[:], wup.t[:, g * 256 + dh * 128:g * 256 + (dh + 1) * 128],
                                                                   glow.t[:], start=True, stop=True)], [wup, glow], [ps_z[dh]])
                    P.do("act", lambda e, dh=dh, g=g: e.activation(out=spb.t[:, dh, :], in_=ps_z[dh].t[:], func=AF.Exp,
                                                                  bias=nbgk.t[:, g * 2 + dh:g * 2 + dh + 1], scale=-1.0),
                         [ps_z[dh], nbgk], [spb])
                P.do("act", lambda e: e.activation(out=spb.t[:], in_=spb.t[:], func=AF.Ln, bias=1.0), [spb], [spb])
                for dh in range(2):
                    P.do("dve", lambda e, dh=dh: e.tensor_tensor_scan(out=bsp.t[:, dh, :], data0=rmask.t[:], data1=spb.t[:, dh, :],
                                                                     initial=0.0, op0=ALU.mult, op1=ALU.add), [spb, rmask], [bsp])
                P.do("act", lambda e: e.activation(out=eb.t[:], in_=bsp.t[:], func=AF.Exp, scale=-1.0 / 16), [bsp], [eb])
                P.do("act", lambda e: e.activation(out=enb.t[:], in_=bsp.t[:], func=AF.Exp, scale=1.0 / 16), [bsp], [enb])
                bv = bsp.t[:].rearrange("p d (c t) -> p (d c) t", t=128)
                dv = dd.t[:].rearrange("p d (c t) -> p (d c) t", t=128)
                P.do("dve", lambda e, bv=bv, dv=dv: e.tensor_tensor(out=dv, in0=bv[:, :, 127:128].broadcast_to([128, 8, 128]),
                                                                   in1=bv, op=ALU.subtract), [bsp], [dd])
                P.do("act", lambda e: e.activation(out=ekd.t[:], in_=dd.t[:], func=AF.Exp, scale=-1.0 / 16), [dd], [ekd])
                P.do("dve", lambda e, gq=gq: e.scalar_tensor_tensor(out=qe.t[:], in0=gq.t[:], scalar=1.0 / 16, in1=eb.t[:],
                                                                   op0=ALU.mult, op1=ALU.mult), [gq, eb], [qe])
                P.do("dve", lambda e, gk=gk: e.tensor_tensor(out=ke.t[:], in0=gk.t[:], in1=enb.t[:], op=ALU.mult), [gk, enb], [ke])
                P.do("dve", lambda e, gk=gk: e.tensor_tensor(out=kdT.t[:], in0=gk.t[:], in1=ekd.t[:], op=ALU.mult), [gk, ekd], [kdT])
                for half in range(2):
                    P.mm([(lambda e, half=half, b=b, ec=ec, gv=gv: e.transpose(
                        ps_t.t[:, half, (b * 4 + ec) * 128:(b * 4 + ec + 1) * 128],
                        gv.t[:, ec, (half * 2 + b) * 128:(half * 2 + b + 1) * 128], G.ident.t[:]))
                        for b in range(2) for ec in range(4)], [gv, G.ident], [ps_t])
                P.do("act", lambda e: e.copy(out=vtok.t[:].rearrange("p b e -> p (b e)"),
                                             in_=ps_t.t[:].rearrange("p h x -> p (h x)")), [ps_t], [vtok])
                P.mm([(lambda e, b=b, dh=dh: e.transpose(ps_t.t[:, 0, (b * 2 + dh) * 128:(b * 2 + dh + 1) * 128],
                                                        kdT.t[:, dh, b * 128:(b + 1) * 128], G.ident.t[:]))
                      for b in range(4) for dh in range(2)], [kdT, G.ident], [ps_t])
                P.do("act", lambda e: e.copy(out=kdtok.t[:].rearrange("p b d -> p (b d)"), in_=ps_t.t[:, 0, :]), [ps_t], [kdtok])
                for b in range(4):
                    ts = slice(b * 128, (b + 1) * 128)
                    P.mm([(lambda e, dh=dh, ts=ts: e.matmul(ps_sc.t[:], ke.t[:, dh, ts], qe.t[:, dh, ts], start=(dh == 0),
                                                           stop=(dh == 1))) for dh in range(2)], [ke, qe], [ps_sc])
                    scm = scms.get()
                    P.do("dve", lambda e, scm=scm: e.tensor_tensor(out=scm.t[:], in0=ps_sc.t[:], in1=tri.t[:], op=ALU.mult),
                         [ps_sc, tri], [scm])
                    mms = []
                    for ec in range(4):
                        es = slice(ec * 128, (ec + 1) * 128)
                        mms.append(lambda e, ec=ec, es=es, ts=ts: e.matmul(ps_o.t[:, ec, :], Sbf.t[:, 0, es], qe.t[:, 0, ts],
                                                                          start=True, stop=False))
                        mms.append(lambda e, ec=ec, es=es, ts=ts: e.matmul(ps_o.t[:, ec, :], Sbf.t[:, 1, es], qe.t[:, 1, ts],
                                                                          start=False, stop=False))
                        mms.append(lambda e, ec=ec, es=es, b=b, scm=scm: e.matmul(ps_o.t[:, ec, :], vtok.t[:, b, es], scm.t[:],
                                                                                 start=False, stop=True))
                    P.mm(mms, [Sbf, qe, vtok, scm], [ps_o])
                    P.do("act", lambda e, ts=ts: e.copy(out=og.t[:, :, ts], in_=ps_o.t[:]), [ps_o], [og])
                    for dh in range(2):
                        P.mm([lambda e, dh=dh, b=b: e.matmul(ps_su[dh].t[:], kdtok.t[:, b, dh * 128:(dh + 1) * 128], vtok.t[:, b, :],
                                                            start=True, stop=True)], [kdtok, vtok], [ps_su[dh]])
                    for dh in range(2):
                        col = b * 128 + 127
                        P.do("dve", lambda e, dh=dh, col=col: e.scalar_tensor_tensor(
                            out=Sst.t[:, dh, :], in0=Sst.t[:, dh, :], scalar=eb.t[:, dh, col:col + 1], in1=ps_su[dh].t[:],
                            op0=ALU.mult, op1=ALU.add), [Sst, eb, ps_su[dh]], [Sst])
                    P.do("pool", lambda e: e.tensor_copy(out=Sbf.t[:], in_=Sst.t[:]), [Sst], [Sbf])
                P.do("act", lambda e: e.activation(out=sq.t[:], in_=og.t[:], func=AF.Square), [og], [sq])
                P.mm([(lambda e, ec=ec: e.matmul(ps_z[0].t[:], ones_f.t[:], sq.t[:, ec, :], start=(ec == 0), stop=(ec == 3)))
                      for ec in range(4)], [sq, ones_f], [ps_z[0]])
                P.do("act", lambda e: e.activation(out=rstd.t[:], in_=ps_z[0].t[:], func=AF.Ln, bias=EPS, scale=1.0 / 512),
                     [ps_z[0]], [rstd])
                P.do("act", lambda e: e.activation(out=rstd.t[:], in_=rstd.t[:], func=AF.Exp, scale=-0.5), [rstd], [rstd])
                P.do("act", lambda e, gg=gg: e.activation(out=sig.t[:], in_=gg.t[:], func=AF.Exp, scale=-1.0), [gg], [sig])
                P.do("dve", lambda e: e.tensor_scalar(out=sig.t[:], in0=sig.t[:], scalar1=1.0, scalar2=None, op0=ALU.add), [sig], [sig])
                P.do("dve", lambda e: e.reciprocal(out=sig.t[:], in_=sig.t[:]), [sig], [sig])
                P.do("dve", lambda e, gg=gg: e.tensor_tensor(out=sig.t[:], in0=sig.t[:], in1=gg.t[:], op=ALU.mult), [sig, gg], [sig])
                for ec in range(4):
                    P.do("dve", lambda e, ec=ec: e.scalar_tensor_tensor(out=t1.t[:, ec, :], in0=og.t[:, ec, :],
                                                                       scalar=ggla.t[:, ec:ec + 1], in1=rstd.t[:],
                                                                       op0=ALU.mult, op1=ALU.mult), [og, rstd, ggla], [t1])
                P.do("dve", lambda e: e.tensor_tensor(out=ob.t[:], in0=t1.t[:], in1=sig.t[:], op=ALU.mult), [t1, sig], [ob])
                r0 = 2048 + g * 512
                dst = G.mixT.t[r0:r0 + 512, c0:c0 + 512].rearrange("(c p) t -> p c t", p=128)
                P.dma("sp", lambda e, dst=dst: e.dma_start(out=dst, in_=ob.t[:]), [ob], [], ob)
        P.barrier()


def _layer_norm_tiles(P, st_, zs, gsrc, bsrc, gq, bq, junk, stat):
    for z in zs:
        sc = stat.get()
        P.do("dve", lambda e, sc=sc: e.memset(sc.t[:], 0.0), [], [sc])
        P.do("act", lambda e, z=z, sc=sc: e.activation(out=junk.t[:], in_=z.t[:], func=AF.Identity, accum_out=sc.t[:, 0:1]),
             [z], [junk, sc])
        P.do("act", lambda e, z=z, sc=sc: e.activation(out=junk.t[:], in_=z.t[:], func=AF.Square, accum_out=sc.t[:, 1:2]),
             [z], [junk, sc])
        P.do("dve", lambda e, sc=sc: e.tensor_scalar(out=sc.t[:, 2:4], in0=sc.t[:, 0:2], scalar1=1.0 / D, scalar2=None,
                                                   op0=ALU.mult), [sc], [sc])
        P.do("dve", lambda e, sc=sc: e.tensor_tensor(out=sc.t[:, 4:5], in0=sc.t[:, 2:3], in1=sc.t[:, 2:3], op=ALU.mult), [sc], [sc])
        P.do("dve", lambda e, sc=sc: e.tensor_tensor(out=sc.t[:, 5:6], in0=sc.t[:, 3:4], in1=sc.t[:, 4:5], op=ALU.subtract),
             [sc], [sc])
        P.do("act", lambda e, sc=sc: e.activation(out=sc.t[:, 6:7], in_=sc.t[:, 5:6], func=AF.Ln, bias=EPS), [sc], [sc])
        P.do("act", lambda e, sc=sc: e.activation(out=sc.t[:, 7:8], in_=sc.t[:, 6:7], func=AF.Exp, scale=-0.5), [sc], [sc])
        P.do("dve", lambda e, z=z, sc=sc: e.tensor_scalar(out=z.t[:], in0=z.t[:], scalar1=sc.t[:, 2:3], scalar2=sc.t[:, 7:8],
                                                        op0=ALU.subtract, op1=ALU.mult), [z, sc], [z])
    for qtr in range(4):
        cs = slice(qtr * 1024, (qtr + 1) * 1024)
        P.dma("sp", lambda e, cs=cs: e.dma_start(out=gq.t[:], in_=gsrc.t[cs].partition_broadcast(128)), [], [gq], gq)
        P.dma("sp", lambda e, cs=cs: e.dma_start(out=bq.t[:], in_=bsrc.t[cs].partition_broadcast(128)), [], [bq], bq)
        for z in zs:
            P.do("pool", lambda e, z=z, cs=cs: e.tensor_tensor(out=z.t[:, cs], in0=z.t[:, cs], in1=gq.t[:], op=ALU.mult), [z, gq], [z])
            P.do("pool", lambda e, z=z, cs=cs: e.tensor_tensor(out=z.t[:, cs], in0=z.t[:, cs], in1=bq.t[:], op=ALU.add), [z, bq], [z])


def phase_C(P, G, l, x_res, x_out, groups=range(8)):
    QCH = [(0, 22), (22, 44), (44, 66), (66, 86)]
    with ExitStack() as st:
        zs = [P.sb([128, D], F32, st) for _ in range(4)]
        big = P.sb([128, 32, 512], BF16, st)
        h1T = P.sb([128, 32, 512], BF16, st)
        wp = RR([P.sb([128, 4096], BF16, st) for _ in range(5)])
        gq = P.sb([128, 1024], F32, st)
        bq = P.sb([128, 1024], F32, st)
        rbs = RR([P.sb([128, 512], F32, st) for _ in range(3)])
        sgs = RR([P.sb([128, 512], F32, st) for _ in range(2)])
        stat = RR([P.sb([128, 8], F32, st) for _ in range(2)])
        junk = P.sb([128, D], BF16, st)
        xb = P.sb([128, D], BF16, st)
        pm = RR([P.ps([128, 512], F32, st) for _ in range(6)])
        pt = RR([P.ps([128, 1024], BF16, st) for _ in range(2)])
        ev = 0
        h1tok = [None] * 4
        for tg in groups:
            c0 = tg * 512
            for p4 in range(4):
                src = G.mixT.t[p4 * 1024:(p4 + 1) * 1024, c0:c0 + 512].rearrange("(c p) t -> p c t", p=128)
                P.dma("sp", lambda e, src=src, p4=p4: e.dma_start(out=big.t[:, p4 * 8:(p4 + 1) * 8, :], in_=src), [], [big], big)
            for n in range(8):
                ns = slice(n * 512, (n + 1) * 512)
                accs = [pm.get() for _ in range(4)]
                for half in range(2):
                    ws = []
                    for pc in (half * 2, half * 2 + 1):
                        w = wp.get()
                        src = G.wout[l].t[n, pc * 8:(pc + 1) * 8].rearrange("f p m -> p f m")
                        P.dma("pool", lambda e, w=w, src=src: e.dma_start(out=w.t[:].rearrange("p (f m) -> p f m", m=512), in_=src),
                              [], [w], w)
                        ws.append((pc, w))
                    mms = []
                    for pc, w in ws:
                        wv = w.t[:].rearrange("p (f m) -> p f m", m=512)
                        for f8 in range(8):
                            fc = pc * 8 + f8
                            for tt in range(4):
                                mms.append(lambda e, wv=wv, f8=f8, fc=fc, tt=tt, acc=accs[tt]: e.matmul(
                                    acc.t[:], big.t[:, fc, tt * 128:(tt + 1) * 128], wv[:, f8, :], start=(fc == 0), stop=(fc == 31)))
                    P.mm(mms, [big] + [w for _, w in ws], accs)
                for tt in range(4):
                    rb = rbs.get()
                    r0 = c0 + tt * 128
                    P.dma("sp", lambda e, rb=rb, r0=r0, ns=ns: e.dma_start(out=rb.t[:], in_=x_res.t[r0:r0 + 128, ns]), [], [rb], rb)
                    P.do("dve", lambda e, tt=tt, rb=rb, ns=ns, acc=accs[tt]: e.scalar_tensor_tensor(
                        out=zs[tt].t[:, ns], in0=rb.t[:], scalar=ALPHA, in1=acc.t[:], op0=ALU.mult, op1=ALU.add),
                        [rb, accs[tt]], [zs[tt]])
            _layer_norm_tiles(P, st, zs, G.ln1g[l], G.ln1b[l], gq, bq, junk, stat)
            for tt in range(4):
                r0 = c0 + tt * 128
                h1tok[tt] = P.dma("sp", lambda e, tt=tt, r0=r0: e.dma_start(out=G.h1.t[r0:r0 + 128, :], in_=zs[tt].t[:]), [zs[tt]], [], zs[tt])
                P.do("act", lambda e, tt=tt: e.copy(out=xb.t[:], in_=zs[tt].t[:]), [zs[tt]], [xb])
                for k8 in range(4):
                    p = pt.get()
                    P.mm([(lambda e, p=p, k8=k8, j=j: e.transpose(p.t[:, j * 128:(j + 1) * 128],
                                                                 xb.t[:, (k8 * 8 + j) * 128:(k8 * 8 + j + 1) * 128], G.ident.t[:]))
                          for j in range(8)], [xb, G.ident], [p])
                    src = p.t[:].rearrange("p (j t) -> p j t", t=128)
                    dst = h1T.t[:, k8 * 8:(k8 + 1) * 8, tt * 128:(tt + 1) * 128]
                    if ev % 2 == 0:
                        P.do("dve", lambda e, src=src, dst=dst: e.tensor_copy(out=dst, in_=src), [p], [h1T])
                    else:
                        P.do("act", lambda e, src=src, dst=dst: e.copy(out=dst, in_=src), [p], [h1T])
                    ev += 1
            for qi, (j0, j1) in enumerate(QCH):
                for j in range(j0, j1):
                    wg, wu = wp.get(), wp.get()
                    P.dma("pool", lambda e, wg=wg, j=j: e.dma_start(out=wg.t[:].rearrange("p (k m) -> p k m", m=128), in_=G.wg[l].t[j]),
                          [], [wg], wg)
                    P.dma("pool", lambda e, wu=wu, j=j: e.dma_start(out=wu.t[:].rearrange("p (k m) -> p k m", m=128), in_=G.wu[l].t[j]),
                          [], [wu], wu)
                    pg, pu = pm.get(), pm.get()
                    wgv = wg.t[:].rearrange("p (k m) -> p k m", m=128)
                    wuv = wu.t[:].rearrange("p (k m) -> p k m", m=128)
                    P.mm([(lambda e, kc=kc, wgv=wgv, pg=pg: e.matmul(pg.t[:], wgv[:, kc, :], h1T.t[:, kc, :], start=(kc == 0),
                                                                    stop=(kc == 31))) for kc in range(32)], [wg, h1T], [pg])
                    P.mm([(lambda e, kc=kc, wuv=wuv, pu=pu: e.matmul(pu.t[:], wuv[:, kc, :], h1T.t[:, kc, :], start=(kc == 0),
                                                                    stop=(kc == 31))) for kc in range(32)], [wu, h1T], [pu])
                    sg = sgs.get()
                    P.do("act", lambda e, sg=sg, pg=pg: e.activation(out=sg.t[:], in_=pg.t[:], func=AF.Silu), [pg], [sg])
                    P.do("dve", lambda e, sg=sg, pu=pu, jj=j - j0: e.tensor_tensor(out=big.t[:, jj, :], in0=pu.t[:], in1=sg.t[:],
                                                                                 op=ALU.mult), [pu, sg], [big])
                for n in range(8):
                    ns = slice(n * 512, (n + 1) * 512)
                    accs = [pm.get() for _ in range(4)]
                    ja = j0
                    while ja < j1:
                        jb = min(ja + 8, j1)
                        w = wp.get()
                        src = G.wd[l].t[n, ja:jb].rearrange("f p m -> p f m")
                        nj = jb - ja
                        P.dma("pool", lambda e, w=w, src=src, nj=nj: e.dma_start(
                            out=w.t[:, 0:nj * 512].rearrange("p (f m) -> p f m", m=512), in_=src), [], [w], w)
                        wv = w.t[:].rearrange("p (f m) -> p f m", m=512)
                        mms = []
                        for j in range(ja, jb):
                            for tt in range(4):
                                mms.append(lambda e, wv=wv, j=j, ja=ja, tt=tt, acc=accs[tt], j0=j0, j1=j1: e.matmul(
                                    acc.t[:], big.t[:, j - j0, tt * 128:(tt + 1) * 128], wv[:, j - ja, :], start=(j == j0),
                                    stop=(j == j1 - 1)))
                        P.mm(mms, [big, w], accs)
                        ja = jb
                    for tt in range(4):
                        if qi == 0:
                            rb = rbs.get()
                            r0 = c0 + tt * 128
                            P.dma("sp", lambda e, rb=rb, r0=r0, ns=ns: e.dma_start(out=rb.t[:], in_=G.h1.t[r0:r0 + 128, ns]),
                                  [], [rb], rb, extra=[h1tok[tt]])
                            P.do("dve", lambda e, tt=tt, rb=rb, ns=ns, acc=accs[tt]: e.scalar_tensor_tensor(
                                out=zs[tt].t[:, ns], in0=rb.t[:], scalar=ALPHA, in1=acc.t[:], op0=ALU.mult, op1=ALU.add),
                                [rb, accs[tt]], [zs[tt]])
                        else:
                            P.do("dve", lambda e, tt=tt, ns=ns, acc=accs[tt]: e.tensor_tensor(
                                out=zs[tt].t[:, ns], in0=acc.t[:], in1=zs[tt].t[:, ns], op=ALU.add), [accs[tt], zs[tt]], [zs[tt]])
            _layer_norm_tiles(P, st, zs, G.ln2g[l], G.ln2b[l], gq, bq, junk, stat)
            for tt in range(4):
                r0 = c0 + tt * 128
                P.dma("sp", lambda e, tt=tt, r0=r0: e.dma_start(out=x_out.t[r0:r0 + 128, :], in_=zs[tt].t[:]), [zs[tt]], [], zs[tt])
        P.barrier()


class Globals:
    pass


def build_program(mode="full"):
    nc = bass.Bass("TRN2", target_bir_lowering=False)
    P = Prog(nc)
    G = Globals()
    G.inputs = {}

    def inp(name, shape, dt=F32):
        G.inputs[name] = tuple(shape)
        return P.dram(name, shape, dt, "ExternalInput")

    layers = [0] if mode in ("A0", "AB0", "C0", "L0") else [0, 1]
    need_A = mode in ("A0", "AB0", "L0", "full")
    need_B = mode in ("AB0", "L0", "full")
    need_C = mode in ("C0", "L0", "full")
    G.x = inp("x", [S, D]) if need_A else None
    if need_A:
        G.ln_in_g = inp("ln_in_g", [D])
        G.ln_in_b = inp("ln_in_b", [D])
    G.win, G.wsm, G.b_f, G.w_gk_up, G.bgk, G.gfox, G.ggla = {}, {}, {}, {}, {}, {}, {}
    G.wout, G.ln1g, G.ln1b, G.wg, G.wu, G.wd, G.ln2g, G.ln2b = {}, {}, {}, {}, {}, {}, {}, {}
    for l in layers:
        if need_A:
            G.win[l] = inp(f"win{l}", [96, 128, 32, 128])
            G.wsm[l] = inp(f"wsm{l}", [128, 32, 32])
        if need_B:
            G.b_f[l] = inp(f"b_f{l}", [16, 1])
            G.w_gk_up[l] = inp(f"wgkup{l}", [16, 1024])
            G.bgk[l] = inp(f"bgk{l}", [128, 8])
            G.gfox[l] = inp(f"gfox{l}", [128, 16])
            G.ggla[l] = inp(f"ggla{l}", [128, 4])
        if need_C:
            G.wout[l] = inp(f"wout{l}", [8, 32, 128, 512])
            G.ln1g[l] = inp(f"ln1g{l}", [D])
            G.ln1b[l] = inp(f"ln1b{l}", [D])
            G.wg[l] = inp(f"wg{l}", [NFF, 128, 32, 128])
            G.wu[l] = inp(f"wu{l}", [NFF, 128, 32, 128])
            G.wd[l] = inp(f"wd{l}", [8, NFF, 128, 512])
            G.ln2g[l] = inp(f"ln2g{l}", [D])
            G.ln2b[l] = inp(f"ln2b{l}", [D])
    if mode == "A0":
        G.projT = P.dram("projT", [12288, S], BF16, "ExternalOutput")
        G.projF = P.dram("projF", [32, S], F32, "ExternalOutput")
        G.xn = P.dram("xn", [S, D], F32, "ExternalOutput")
    else:
        G.projT = P.dram("projT", [12288, S], BF16)
        G.projF = P.dram("projF", [32, S], F32)
        G.xn = P.dram("xn", [S, D], F32) if mode != "C0" else inp("xn", [S, D])
    if mode == "AB0":
        G.mixT = P.dram("mixT", [D, S], BF16, "ExternalOutput")
    elif mode == "C0":
        G.mixT = inp("mixT", [D, S], BF16)
    else:
        G.mixT = P.dram("mixT", [D, S], BF16)
    G.h1 = P.dram("h1", [S, D], F32)
    G.x1 = P.dram("x1", [S, D], F32) if mode == "full" else None
    if mode in ("C0", "L0", "full"):
        G.out = P.dram("out", [S, D], F32, "ExternalOutput")
    G.ident = P.sb([128, 128], BF16)
    P.do("pool", lambda e: e.memset(G.ident.t[:], 1.0), [], [G.ident])
    P.do("pool", lambda e: e.affine_select(out=G.ident.t[:], in_=G.ident.t[:], pattern=[[-1, 128]], compare_op=ALU.is_equal,
                                           fill=0.0, base=0, channel_multiplier=1), [G.ident], [G.ident])
    for l in layers:
        last = (l == layers[-1])
        if need_A:
            phase_A(P, G, l, G.x if l == 0 else G.x1, apply_ln=(l == 0))
        if need_B:
            phase_B_fox(P, G, l)
            phase_B_gla(P, G, l)
        if need_C:
            phase_C(P, G, l, G.xn if l == 0 else G.x1, G.out if last else G.x1)
    P.barrier()
    P.finish()
    return nc, G


def host_layouts(inputs, layers=(0, 1), need=("A", "B", "C")):
    m = {}
    if "A" in need:
        m["ln_in_g"] = np.ascontiguousarray(inputs["ln_in_g"], dtype=np.float32)
        m["ln_in_b"] = np.ascontiguousarray(inputs["ln_in_b"], dtype=np.float32)
    for l in layers:
        if "A" in need:
            w = inputs["w_in"][l]
            big = np.concatenate([w[:, 0:6144], w[:, 6160:12304]], axis=1)
            m[f"win{l}"] = np.ascontiguousarray(big.reshape(32, 128, 96, 128).transpose(2, 1, 0, 3))
            small = np.concatenate([w[:, 6144:6160], w[:, 12304:12320]], axis=1)
            m[f"wsm{l}"] = np.ascontiguousarray(small.reshape(32, 128, 32).transpose(1, 0, 2))
        if "B" in need:
            m[f"b_f{l}"] = np.ascontiguousarray(inputs["b_f"][l].reshape(16, 1))
            m[f"wgkup{l}"] = np.ascontiguousarray(inputs["w_gk_up"][l])
            m[f"bgk{l}"] = np.ascontiguousarray(inputs["b_gk"][l].reshape(8, 128).T)
            m[f"gfox{l}"] = np.ascontiguousarray(inputs["fox_norm_g"][l].T)
            m[f"ggla{l}"] = np.ascontiguousarray(inputs["gla_norm_g"][l].reshape(4, 128).T)
        if "C" in need:
            m[f"wout{l}"] = np.ascontiguousarray(inputs["w_out"][l].reshape(32, 128, 8, 512).transpose(2, 0, 1, 3))
            m[f"ln1g{l}"] = np.ascontiguousarray(inputs["ln1_g"][l])
            m[f"ln1b{l}"] = np.ascontiguousarray(inputs["ln1_b"][l])
            m[f"wg{l}"] = np.ascontiguousarray(inputs["w_gate"][l].reshape(32, 128, NFF, 128).transpose(2, 1, 0, 3))
            m[f"wu{l}"] = np.ascontiguousarray(inputs["w_up"][l].reshape(32, 128, NFF, 128).transpose(2, 1, 0, 3))
            m[f"wd{l}"] = np.ascontiguousarray(inputs["w_down"][l].reshape(NFF, 128, 8, 512).transpose(2, 0, 1, 3))
            m[f"ln2g{l}"] = np.ascontiguousarray(inputs["ln2_g"][l])
            m[f"ln2b{l}"] = np.ascontiguousarray(inputs["ln2_b"][l])
    return m


def kernel(**inputs):
    inputs = {k: np.asarray(v) for k, v in inputs.items()}
    nc, G = build_program("full")
    shared = host_layouts(inputs)
    x = np.asarray(inputs["x"], dtype=np.float32)
    in_maps = []
    for b in range(2):
        m = dict(shared)
        m["x"] = np.ascontiguousarray(x[b])
        in_maps.append(m)
    res = run_bass_kernel_spmd(nc, in_maps, core_ids=[0, 1])
    out = np.stack([np.asarray(res.results[b]["out"]) for b in range(2)], axis=0)
    return out.astype(np.float32)
```

```python
from contextlib import ExitStack
import numpy as np
import concourse.bass as bass
import concourse.mybir as mybir
from concourse.bass_utils import run_bass_kernel_spmd

F32 = mybir.dt.float32
BF16 = mybir.dt.bfloat16
AF = mybir.ActivationFunctionType
ALU = mybir.AluOpType

S = 4096
D = 4096
DFF = 11008
NFF = DFF // 128
DEPTH = 2
ALPHA = (2 * DEPTH) ** 0.25
EPS = 1e-5
ENGS = ("pe", "act", "dve", "pool", "sp")
FQ0, FK0, FV0, GQ0, GK0, GV0, GG0 = 0, 2048, 4096, 6144, 7168, 8192, 10240


class Buf:
    __slots__ = ("t", "w", "r", "sem")

    def __init__(self, t):
        self.t = t
        self.w = None
        self.r = []
        self.sem = None


class Prog:
    def __init__(self, nc):
        self.nc = nc
        self.es = ExitStack()
        self.q = {e: [] for e in ENGS}
        self.cnt = {e: 0 for e in ENGS}
        self.esem = {e: self.es.enter_context(nc.semaphore("es_" + e)) for e in ENGS}
        self.dsems = []
        self.waited = {e: {} for e in ENGS}
        self.n = 0
        self.allbufs = []
        self.free_sems = []

    def _nm(self, p):
        self.n += 1
        return f"{p}{self.n}"

    def sb(self, shape, dt, stack=None):
        b = Buf((stack or self.es).enter_context(self.nc.sbuf_tensor(self._nm("sb"), list(shape), dt)))
        self.allbufs.append(b)
        return b

    def ps(self, shape, dt, stack=None):
        b = Buf((stack or self.es).enter_context(self.nc.psum_tensor(self._nm("ps"), list(shape), dt)))
        self.allbufs.append(b)
        return b

    def dram(self, name, shape, dt, kind="Internal"):
        b = Buf(self.nc.dram_tensor(name, list(shape), dt, kind=kind).ap())
        self.allbufs.append(b)
        return b

    def sub(self, buf):
        b = Buf(buf.t)
        self.allbufs.append(b)
        return b

    def _wait(self, e, tok):
        if tok is None:
            return
        kind, key, val = tok
        if kind == "e":
            if e == "pe" and key == "pe":
                return
            sem = self.esem[key]
        else:
            sem = self.dsems[key][0]
            val = self.dsems[key][1]
        k = (kind, key)
        if self.waited[e].get(k, 0) >= val:
            return
        self.waited[e][k] = val
        self.q[e].append(("w", sem, val))

    def _deps(self, e, reads, writes, extra):
        for b in reads:
            self._wait(e, b.w)
        for b in writes:
            self._wait(e, b.w)
            for t in b.r:
                self._wait(e, t)
        for t in extra:
            self._wait(e, t)

    def _commit(self, tok, reads, writes):
        for b in reads:
            b.r.append(tok)
            if len(b.r) > 24:
                b.r = b.r[-24:]
        for b in writes:
            b.w = tok
            b.r = []

    def do(self, e, fn, reads=(), writes=(), extra=(), mark=True):
        self._deps(e, reads, writes, extra)
        tok = None
        if mark:
            self.cnt[e] += 1
            tok = ("e", e, self.cnt[e])
        self.q[e].append(("o", fn, self.esem[e] if mark else None, 1))
        if tok:
            self._commit(tok, reads, writes)
        return tok

    def mm(self, mms, reads, writes):
        self._deps("pe", reads, writes, ())
        for fn in mms[:-1]:
            self.q["pe"].append(("o", fn, None, 1))
        self.cnt["pe"] += 1
        tok = ("e", "pe", self.cnt["pe"])
        self.q["pe"].append(("o", mms[-1], self.esem["pe"], 1))
        self._commit(tok, reads, writes)
        return tok

    def dma(self, e, fn, reads, writes, sembuf, extra=()):
        self._deps(e, reads, writes, extra)
        if sembuf.sem is None:
            if self.free_sems:
                sembuf.sem = self.free_sems.pop()
            else:
                self.dsems.append([self.es.enter_context(self.nc.semaphore(self._nm("ds"))), 0])
                sembuf.sem = len(self.dsems) - 1
        rec = self.dsems[sembuf.sem]
        rec[1] += 16
        tok = ("d", sembuf.sem, rec[1])
        self.q[e].append(("o", fn, rec[0], 16))
        self._commit(tok, reads, writes)
        return tok

    def barrier(self):
        for e in ENGS:
            for k in ("pe", "act", "dve", "pool"):
                if self.cnt[k] > 0 and not (e == k):
                    self._wait(e, ("e", k, self.cnt[k]))
            for i, rec in enumerate(self.dsems):
                if rec[1] > 0:
                    self._wait(e, ("d", i, rec[1]))
        for b in self.allbufs:
            b.w = None
            b.r = []
            if b.sem is not None:
                self.free_sems.append(b.sem)
                b.sem = None

    def finish(self):
        nc, q = self.nc, self.q

        def run(eng, lst):
            for it in lst:
                if it[0] == "w":
                    eng.wait_ge(it[1], it[2])
                else:
                    ins = it[1](eng)
                    if it[2] is not None:
                        ins.then_inc(it[2], it[3])

        with nc.Block() as block:
            @block.tensor
            def _(eng):
                run(eng, q["pe"])

            @block.scalar
            def _(eng):
                run(eng, q["act"])

            @block.vector
            def _(eng):
                run(eng, q["dve"])

            @block.gpsimd
            def _(eng):
                run(eng, q["pool"])

            @block.sync
            def _(eng):
                run(eng, q["sp"])
        self.es.close()


class RR:
    def __init__(self, bufs):
        self.b = bufs
        self.i = 0

    def get(self):
        b = self.b[self.i % len(self.b)]
        self.i += 1
        return b


def phase_A(P, G, l, x_src, apply_ln, stage_limit=None):
    TGA = 1024
    with ExitStack() as st:
        xnT = P.sb([128, 32, TGA], BF16, st)
        xts = RR([P.sb([128, D], F32, st) for _ in range(2)])
        xbs = RR([P.sb([128, D], BF16, st) for _ in range(2)])
        wp = RR([P.sb([128, 32, 128], BF16, st) for _ in range(4)])
        wsm = P.sb([128, 32, 32], BF16, st)
        stg = RR([P.sb([128, 512], BF16, st) for _ in range(4)])
        stgf = RR([P.sb([32, 512], F32, st) for _ in range(2)])
        stat = RR([P.sb([128, 8], F32, st) for _ in range(2)])
        junk = P.sb([128, D], BF16, st)
        pt = RR([P.ps([128, 1024], BF16, st) for _ in range(2)])
        pm = RR([P.ps([128, 512], F32, st) for _ in range(4)])
        if apply_ln:
            gbc = P.sb([128, D], F32, st)
            bbc = P.sb([128, D], F32, st)
            P.dma("sp", lambda e: e.dma_start(out=gbc.t[:], in_=G.ln_in_g.t.partition_broadcast(128)), [], [gbc], gbc)
            P.dma("sp", lambda e: e.dma_start(out=bbc.t[:], in_=G.ln_in_b.t.partition_broadcast(128)), [], [bbc], bbc)
        P.dma("pool", lambda e: e.dma_start(out=wsm.t[:], in_=G.wsm[l].t), [], [wsm], wsm)
        ev = 0
        for tg in range(S // TGA):
            if stage_limit is not None and tg >= stage_limit:
                break
            for tt in range(TGA // 128):
                r0 = tg * TGA + tt * 128
                xt = xts.get()
                xb = xbs.get()
                P.dma("sp", lambda e, xt=xt, r0=r0: e.dma_start(out=xt.t[:], in_=x_src.t[r0:r0 + 128, :]), [], [xt], xt)
                if apply_ln:
                    sc = stat.get()
                    P.do("dve", lambda e, sc=sc: e.memset(sc.t[:], 0.0), [], [sc])
                    P.do("act", lambda e, xt=xt, sc=sc: e.activation(out=junk.t[:], in_=xt.t[:], func=AF.Identity,
                                                                   accum_out=sc.t[:, 0:1]), [xt], [junk, sc])
                    P.do("act", lambda e, xt=xt, sc=sc: e.activation(out=junk.t[:], in_=xt.t[:], func=AF.Square,
                                                                   accum_out=sc.t[:, 1:2]), [xt], [junk, sc])
                    P.do("dve", lambda e, sc=sc: e.tensor_scalar(out=sc.t[:, 2:4], in0=sc.t[:, 0:2], scalar1=1.0 / D,
                                                               scalar2=None, op0=ALU.mult), [sc], [sc])
                    P.do("dve", lambda e, sc=sc: e.tensor_tensor(out=sc.t[:, 4:5], in0=sc.t[:, 2:3], in1=sc.t[:, 2:3],
                                                               op=ALU.mult), [sc], [sc])
                    P.do("dve", lambda e, sc=sc: e.tensor_tensor(out=sc.t[:, 5:6], in0=sc.t[:, 3:4], in1=sc.t[:, 4:5],
                                                               op=ALU.subtract), [sc], [sc])
                    P.do("act", lambda e, sc=sc: e.activation(out=sc.t[:, 6:7], in_=sc.t[:, 5:6], func=AF.Ln, bias=EPS),
                         [sc], [sc])
                    P.do("act", lambda e, sc=sc: e.activation(out=sc.t[:, 7:8], in_=sc.t[:, 6:7], func=AF.Exp, scale=-0.5),
                         [sc], [sc])
                    P.do("dve", lambda e, xt=xt, sc=sc: e.scalar_tensor_tensor(out=xt.t[:], in0=xt.t[:], scalar=sc.t[:, 2:3],
                                                                             in1=gbc.t[:], op0=ALU.subtract, op1=ALU.mult),
                         [xt, sc, gbc], [xt])
                    P.do("dve", lambda e, xt=xt, sc=sc: e.scalar_tensor_tensor(out=xt.t[:], in0=xt.t[:], scalar=sc.t[:, 7:8],
                                                                             in1=bbc.t[:], op0=ALU.mult, op1=ALU.add),
                         [xt, sc, bbc], [xt])
                    P.dma("sp", lambda e, xt=xt, r0=r0: e.dma_start(out=G.xn.t[r0:r0 + 128, :], in_=xt.t[:]), [xt], [], xt)
                P.do("act", lambda e, xt=xt, xb=xb: e.copy(out=xb.t[:], in_=xt.t[:]), [xt], [xb])
                for k8 in range(4):
                    p = pt.get()
                    P.mm([(lambda e, p=p, xb=xb, k8=k8, j=j: e.transpose(p.t[:, j * 128:(j + 1) * 128],
                                                                       xb.t[:, (k8 * 8 + j) * 128:(k8 * 8 + j + 1) * 128],
                                                                       G.ident.t[:])) for j in range(8)], [xb, G.ident], [p])
                    eng = "dve" if (ev % 2 == 0) else "act"
                    ev += 1
                    src = p.t[:].rearrange("p (j t) -> p j t", t=128)
                    dst = xnT.t[:, k8 * 8:(k8 + 1) * 8, tt * 128:(tt + 1) * 128]
                    if eng == "dve":
                        P.do("dve", lambda e, src=src, dst=dst: e.tensor_copy(out=dst, in_=src), [p], [xnT])
                    else:
                        P.do("act", lambda e, src=src, dst=dst: e.copy(out=dst, in_=src), [p], [xnT])
            for mt in range(97):
                small = (mt == 96)
                if small:
                    w = wsm
                    M = 32
                else:
                    w = wp.get()
                    M = 128
                    P.dma("pool", lambda e, w=w, mt=mt: e.dma_start(out=w.t[:], in_=G.win[l].t[mt]), [], [w], w)
                for hf in range(TGA // 512):
                    acc = pm.get()
                    P.mm([(lambda e, acc=acc, w=w, kc=kc, hf=hf, M=M: e.matmul(acc.t[0:M, :], w.t[:, kc, 0:M],
                                                                             xnT.t[:, kc, hf * 512:(hf + 1) * 512],
                                                                             start=(kc == 0), stop=(kc == 31)))
                          for kc in range(32)], [w, xnT], [acc])
                    c0 = tg * TGA + hf * 512
                    eng = "dve" if (ev % 2 == 0) else "act"
                    ev += 1
                    if small:
                        sg = stgf.get()
                        P.do("dve", lambda e, sg=sg, acc=acc: e.tensor_copy(out=sg.t[:], in_=acc.t[0:32, :]), [acc], [sg])
                        P.dma("sp", lambda e, sg=sg, c0=c0: e.dma_start(out=G.projF.t[:, c0:c0 + 512], in_=sg.t[:]),
                              [sg], [], sg)
                    else:
                        sg = stg.get()
                        if eng == "dve":
                            P.do("dve", lambda e, sg=sg, acc=acc: e.tensor_copy(out=sg.t[:], in_=acc.t[:]), [acc], [sg])
                        else:
                            P.do("act", lambda e, sg=sg, acc=acc: e.copy(out=sg.t[:], in_=acc.t[:]), [acc], [sg])
                        P.dma("sp", lambda e, sg=sg, c0=c0, mt=mt: e.dma_start(
                            out=G.projT.t[mt * 128:(mt + 1) * 128, c0:c0 + 512], in_=sg.t[:]), [sg], [], sg)
        P.barrier()


def phase_B_fox(P, G, l, heads=range(16)):
    SCALE = 128 ** -0.5
    with ExitStack() as st:
        fl = P.sb([16, S], F32, st)
        csp = P.sb([16, S], F32, st)
        nbf = P.sb([16, 1], F32, st)
        ones16 = P.sb([16, 1], F32, st)
        id16 = P.sb([16, 16], F32, st)
        sel = P.sb([16, 16, 128], F32, st)
        cspcol = P.sb([128, 32, 16], F32, st)
        crefbc = P.sb([128, 16, 8], F32, st)
        gfox = P.sb([128, 16], F32, st)
        ones_bf = P.sb([128, 128], BF16, st)
        ones_f = P.sb([128, 128], F32, st)
        qTs = RR([P.sb([128, S], BF16, st) for _ in range(2)])
        kTs = RR([P.sb([128, S], BF16, st) for _ in range(2)])
        vTs = RR([P.sb([128, S], BF16, st) for _ in range(2)])
        vtoks = RR([P.sb([128, 32, 128], BF16, st) for _ in range(2)])
        nbs = RR([P.sb([128, 32], F32, st) for _ in range(2)])
        pts = RR([P.sb([128, 512], BF16, st) for _ in range(6)])
        rls = RR([P.sb([128, 512], F32, st) for _ in range(3)])
        os_ = RR([P.sb([128, 512], F32, st) for _ in range(3)])
        sqs = RR([P.sb([128, 512], F32, st) for _ in range(3)])
        rstds = RR([P.sb([128, 512], F32, st) for _ in range(3)])
        outs = RR([P.sb([128, 512], BF16, st) for _ in range(3)])
        ps_s = RR([P.ps([128, 512], F32, st) for _ in range(3)])
        ps_o = RR([P.ps([128, 512], F32, st) for _ in range(2)])
        ps_l1 = P.ps([128, 512], F32, st)
        laccs = RR([P.sb([128, 512], F32, st) for _ in range(4)])
        ps_x = P.ps([128, 512], F32, st)
        ps_t = P.ps([128, 1024], BF16, st)

        P.dma("sp", lambda e: e.dma_start(out=fl.t[:], in_=G.projF.t[0:16, :]), [], [fl], fl)
        P.dma("sp", lambda e: e.dma_start(out=nbf.t[:], in_=G.b_f[l].t), [], [nbf], nbf)
        P.dma("sp", lambda e: e.dma_start(out=gfox.t[:], in_=G.gfox[l].t), [], [gfox], gfox)
        P.do("dve", lambda e: e.tensor_scalar(out=nbf.t[:], in0=nbf.t[:], scalar1=-1.0, scalar2=None, op0=ALU.mult), [nbf], [nbf])
        P.do("pool", lambda e: e.memset(ones16.t[:], 1.0), [], [ones16])
        P.do("pool", lambda e: e.memset(ones_bf.t[:], 1.0), [], [ones_bf])
        P.do("pool", lambda e: e.memset(ones_f.t[:], 1.0), [], [ones_f])
        P.do("pool", lambda e: e.memset(id16.t[:], 1.0), [], [id16])
        P.do("pool", lambda e: e.affine_select(out=id16.t[:], in_=id16.t[:], pattern=[[-1, 16]], compare_op=ALU.is_equal,
                                               fill=0.0, base=0, channel_multiplier=1), [id16], [id16])
        P.do("pool", lambda e: e.memset(sel.t[:], 1.0), [], [sel])
        for h in range(16):
            P.do("pool", lambda e, h=h: e.affine_select(out=sel.t[:, h, :], in_=sel.t[:, h, :], pattern=[[0, 128]],
                                                        compare_op=ALU.is_equal, fill=0.0, base=-h, channel_multiplier=1),
                 [sel], [sel])
        P.do("act", lambda e: e.activation(out=fl.t[:], in_=fl.t[:], func=AF.Exp, bias=nbf.t[:], scale=-1.0), [fl, nbf], [fl])
        P.do("act", lambda e: e.activation(out=fl.t[:], in_=fl.t[:], func=AF.Ln, bias=1.0), [fl], [fl])
        P.do("dve", lambda e: e.tensor_tensor_scan(out=csp.t[:], data0=ones16.t[:, 0:1].broadcast_to([16, S]), data1=fl.t[:],
                                                   initial=0.0, op0=ALU.mult, op1=ALU.add), [fl, ones16], [csp])
        px = ps_x.t[:].rearrange("p (b h) -> p b h", h=16)
        P.mm([(lambda e, b=b: e.matmul(px[:, b, :], csp.t[:, b * 128:(b + 1) * 128], id16.t[:], start=True, stop=True))
              for b in range(32)], [csp, id16], [ps_x])
        P.do("dve", lambda e: e.tensor_copy(out=cspcol.t[:], in_=px), [ps_x], [cspcol])
        px2 = ps_x.t[:, 0:128].rearrange("p (h q) -> p h q", q=8)
        cq = csp.t[:].rearrange("p (q t) -> p q t", t=512)[:, :, 0:1].rearrange("p q o -> p (q o)")
        P.mm([(lambda e, h=h: e.matmul(px2[:, h, :], sel.t[:, h, :], cq, start=True, stop=True)) for h in range(16)],
             [csp, sel], [ps_x])
        P.do("dve", lambda e: e.tensor_copy(out=crefbc.t[:], in_=px2), [ps_x], [crefbc])

        heads = list(heads)
        nball = RR([P.sb([128, 8, 32], F32, st) for _ in range(2)])

        def load_head(h):
            qT, kT, vT = qTs.get(), kTs.get(), vTs.get()
            P.dma("sp", lambda e, qT=qT, h=h: e.dma_start(out=qT.t[:], in_=G.projT.t[FQ0 + h * 128:FQ0 + (h + 1) * 128, :]),
                  [], [qT], qT)
            P.dma("sp", lambda e, kT=kT, h=h: e.dma_start(out=kT.t[:], in_=G.projT.t[FK0 + h * 128:FK0 + (h + 1) * 128, :]),
                  [], [kT], kT)
            P.dma("sp", lambda e, vT=vT, h=h: e.dma_start(out=vT.t[:], in_=G.projT.t[FV0 + h * 128:FV0 + (h + 1) * 128, :]),
                  [], [vT], vT)
            return qT, kT, vT

        tiles = []
        for hi, h in enumerate(heads):
            for qb in range(8):
                order = [4 * qb + j for j in range(4)] + list(range(4 * qb))
                for i, kb in enumerate(order):
                    tiles.append(dict(h=h, hi=hi, qb=qb, kb=kb, j=kb - 4 * qb, first=(i == 0), last=(i == len(order) - 1)))
        hctx = {}
        deferred = []

        def start_head(hi):
            h = heads[hi]
            if hi == 0:
                hctx[0] = load_head(h)
            qT, kT, vT = hctx[hi]
            if hi + 1 < len(heads):
                hctx[hi + 1] = load_head(heads[hi + 1])
            vtok = vtoks.get()
            nba = nball.get()
            for b8 in range(4):
                P.mm([(lambda e, b8=b8, j=j, vT=vT: e.transpose(ps_t.t[:, j * 128:(j + 1) * 128],
                                                               vT.t[:, (b8 * 8 + j) * 128:(b8 * 8 + j + 1) * 128],
                                                               G.ident.t[:])) for j in range(8)], [vT, G.ident], [ps_t])
                P.do("dve", lambda e, b8=b8, vtok=vtok: e.tensor_copy(
                    out=vtok.t[:, b8 * 8:(b8 + 1) * 8, :], in_=ps_t.t[:].rearrange("p (j t) -> p j t", t=128)), [ps_t], [vtok])
            for qb in range(8):
                nkb = 4 * qb + 4
                P.do("dve", lambda e, nba=nba, nkb=nkb, h=h, qb=qb: e.tensor_scalar(
                    out=nba.t[:, qb, 0:nkb], in0=cspcol.t[:, 0:nkb, h], scalar1=crefbc.t[:, h, qb:qb + 1], scalar2=None,
                    op0=ALU.subtract), [cspcol, crefbc], [nba])
            return dict(qT=qT, kT=kT, vT=vT, vtok=vtok, nba=nba)

        cur = {}

        def emit_qk(t):
            if t["first"] and t["qb"] == 0:
                cur[t["hi"]] = start_head(t["hi"])
            c = cur[t["hi"]]
            q0 = t["qb"] * 512
            off = 128 * t["j"] if t["j"] > 0 else 0
            nco = 512 - off
            pss = ps_s.get()
            t.update(pss=pss, off=off, nco=nco, q0=q0)
            P.mm([lambda e, pss=pss, kT=c["kT"], qT=c["qT"], kb=t["kb"], q0=q0, off=off, nco=nco: e.matmul(
                pss.t[:, 0:nco], kT.t[:, kb * 128:(kb + 1) * 128], qT.t[:, q0 + off:q0 + 512], start=True, stop=True)],
                [c["kT"], c["qT"]], [pss])

        def epilogue1(c, h, qb, po, pla, plb):
            rl, o, sq = rls.get(), os_.get(), sqs.get()
            rstd, ob = rstds.get(), outs.get()
            q0 = qb * 512

            def st_a():
                mms = [lambda e: e.matmul(ps_l1.t[:], ones_f.t[:], pla.t[:], start=True, stop=(plb is None))]
                rd = [pla, ones_f]
                if plb is not None:
                    mms.append(lambda e: e.matmul(ps_l1.t[:], ones_f.t[:], plb.t[:], start=False, stop=True))
                    rd.append(plb)
                P.mm(mms, rd, [ps_l1])
                deferred.append([2, st_b])

            def st_b():
                P.do("act", lambda e: e.activation(out=rl.t[:], in_=ps_l1.t[:], func=AF.Ln), [ps_l1], [rl])
                P.do("act", lambda e: e.activation(out=rl.t[:], in_=rl.t[:], func=AF.Exp, scale=-1.0), [rl], [rl])
                deferred.append([1, st_b2])

            def st_b2():
                P.do("dve", lambda e: e.tensor_tensor(out=o.t[:], in0=po.t[:], in1=rl.t[:], op=ALU.mult), [po, rl], [o])
                deferred.append([2, st_c])

            def st_c():
                P.do("act", lambda e: e.activation(out=sq.t[:], in_=o.t[:], func=AF.Square), [o], [sq])
                deferred.append([2, st_d])

            def st_d():
                P.mm([lambda e: e.matmul(ps_x.t[:], ones_f.t[:], sq.t[:], start=True, stop=True)], [sq, ones_f], [ps_x])
                deferred.append([2, st_e])

            def st_e():
                P.do("act", lambda e: e.activation(out=rstd.t[:], in_=ps_x.t[:], func=AF.Ln, bias=EPS, scale=1.0 / 128),
                     [ps_x], [rstd])
                P.do("act", lambda e: e.activation(out=rstd.t[:], in_=rstd.t[:], func=AF.Exp, scale=-0.5), [rstd], [rstd])
                deferred.append([2, st_f])

            def st_f():
                P.do("dve", lambda e: e.scalar_tensor_tensor(out=ob.t[:], in0=o.t[:], scalar=gfox.t[:, h:h + 1], in1=rstd.t[:],
                                                             op0=ALU.mult, op1=ALU.mult), [o, rstd, gfox], [ob])
                P.dma("sp", lambda e: e.dma_start(out=G.mixT.t[h * 128:(h + 1) * 128, q0:q0 + 512], in_=ob.t[:]), [ob], [], ob)
            st_a()

        def emit_rest(t):
            c = cur[t["hi"]]
            if t["first"]:
                c["po"], c["pla"], c["plb"] = ps_o.get(), laccs.get(), laccs.get()
                c["na"], c["nb"] = 0, 0
            po = c["po"]
            pss, off, nco, kb = t["pss"], t["off"], t["nco"], t["kb"]
            pt = pts.get()
            nba = c["nba"]
            P.do("act", lambda e, pt=pt, pss=pss, nba=nba, kb=kb, nco=nco, qb=t["qb"]: e.activation(
                out=pt.t[:, 0:nco], in_=pss.t[:, 0:nco], func=AF.Exp, bias=nba.t[:, qb, kb:kb + 1], scale=SCALE),
                [pss, nba], [pt])
            if t["j"] >= 0:
                P.do("pool", lambda e, pt=pt: e.affine_select(out=pt.t[:, 0:128], in_=pt.t[:, 0:128], pattern=[[1, 128]],
                                                            compare_op=ALU.is_ge, fill=0.0, base=0,
                                                            channel_multiplier=-1), [pt], [pt])
            first, last = t["first"], t["last"]
            vtok = c["vtok"]
            P.mm([lambda e, po=po, vtok=vtok, pt=pt, kb=kb, off=off, nco=nco, first=first, last=last: e.matmul(
                po.t[:, off:512], vtok.t[:, kb, :], pt.t[:, 0:nco], start=first, stop=last)], [vtok, pt], [po])
            use_b = (c["na"] > c["nb"])
            eng = "pool" if use_b else "dve"
            pl = c["plb"] if use_b else c["pla"]
            cnt = c["nb"] if use_b else c["na"]
            if cnt == 0:
                if off > 0:
                    P.do(eng, lambda e, pl=pl, off=off: e.memset(pl.t[:, 0:off], 0.0), [], [pl])
                P.do(eng, lambda e, pl=pl, pt=pt, off=off, nco=nco: e.tensor_copy(out=pl.t[:, off:512], in_=pt.t[:, 0:nco]), [pt], [pl])
            else:
                P.do(eng, lambda e, pl=pl, pt=pt, off=off, nco=nco: e.tensor_tensor(
                    out=pl.t[:, off:512], in0=pl.t[:, off:512], in1=pt.t[:, 0:nco], op=ALU.add), [pt, pl], [pl])
            if use_b:
                c["nb"] += 1
            else:
                c["na"] += 1
            if last:
                deferred.append([2, lambda c=c, h=t["h"], qb=t["qb"], po=po, pla=c["pla"], plb=(c["plb"] if c["nb"] > 0 else None):
                                 epilogue1(c, h, qb, po, pla, plb)])

        emit_qk(tiles[0])
        if len(tiles) > 1:
            emit_qk(tiles[1])
        for i, t in enumerate(tiles):
            if i + 2 < len(tiles):
                emit_qk(tiles[i + 2])
            emit_rest(t)
            for dfr in list(deferred):
                dfr[0] -= 1
                if dfr[0] <= 0:
                    deferred.remove(dfr)
                    dfr[1]()
        for dfr in deferred:
            dfr[1]()

        P.barrier()


def phase_B_gla(P, G, l, heads=range(4)):
    with ExitStack() as st:
        wup = P.sb([16, 1024], F32, st)
        nbgk = P.sb([128, 8], F32, st)
        ggla = P.sb([128, 4], F32, st)
        rmask = P.sb([128, 512], F32, st)
        tri = P.sb([128, 128], F32, st)
        ones_f = P.sb([128, 128], F32, st)
        Sst = P.sb([128, 2, 512], F32, st)
        Sbf = P.sb([128, 2, 512], BF16, st)
        gqs = RR([P.sb([128, 2, 512], BF16, st) for _ in range(2)])
        gks = RR([P.sb([128, 2, 512], BF16, st) for _ in range(2)])
        gvs = RR([P.sb([128, 4, 512], BF16, st) for _ in range(2)])
        ggs = RR([P.sb([128, 4, 512], BF16, st) for _ in range(3)])
        glows = RR([P.sb([16, 512], F32, st) for _ in range(2)])
        spb = P.sb([128, 2, 512], F32, st)
        bsp = P.sb([128, 2, 512], F32, st)
        dd = P.sb([128, 2, 512], F32, st)
        enb = P.sb([128, 2, 512], F32, st)
        kdT = P.sb([128, 2, 512], BF16, st)
        ebs = [P.sb([128, 2, 512], F32, st) for _ in range(2)]
        qes = [P.sb([128, 2, 512], BF16, st) for _ in range(2)]
        kes = [P.sb([128, 2, 512], BF16, st) for _ in range(2)]
        kdtoks = [P.sb([128, 4, 256], BF16, st) for _ in range(2)]
        vtoks = [P.sb([128, 4, 512], BF16, st) for _ in range(2)]
        ogs = [P.sb([128, 4, 512], F32, st) for _ in range(2)]
        scms = RR([P.sb([128, 128], BF16, st) for _ in range(2)])
        sq = P.sb([128, 4, 512], F32, st)
        rstd = P.sb([128, 512], F32, st)
        sig = P.sb([128, 4, 512], F32, st)
        ob = P.sb([128, 4, 512], BF16, st)
        ps_z = [P.ps([128, 512], F32, st) for _ in range(2)]
        ps_t = P.ps([128, 1024], BF16, st)
        ps_ss = P.ps([128, 512], F32, st)
        ps_sc = P.ps([128, 128], F32, st)
        ps_o = P.ps([128, 4, 128], F32, st)
        ps_su = [P.ps([128, 512], F32, st) for _ in range(2)]

        P.dma("sp", lambda e: e.dma_start(out=wup.t[:], in_=G.w_gk_up[l].t), [], [wup], wup)
        P.dma("sp", lambda e: e.dma_start(out=nbgk.t[:], in_=G.bgk[l].t), [], [nbgk], nbgk)
        P.dma("sp", lambda e: e.dma_start(out=ggla.t[:], in_=G.ggla[l].t), [], [ggla], ggla)
        P.do("dve", lambda e: e.tensor_scalar(out=nbgk.t[:], in0=nbgk.t[:], scalar1=-1.0, scalar2=None, op0=ALU.mult), [nbgk], [nbgk])
        P.do("pool", lambda e: e.memset(rmask.t[:], 1.0), [], [rmask])
        P.do("pool", lambda e: e.memset(rmask.t[:].rearrange("p (c t) -> p c t", t=128)[:, :, 0:1], 0.0), [rmask], [rmask])
        P.do("pool", lambda e: e.memset(tri.t[:], 1.0), [], [tri])
        P.do("pool", lambda e: e.affine_select(out=tri.t[:], in_=tri.t[:], pattern=[[1, 128]], compare_op=ALU.is_ge, fill=0.0,
                                               base=0, channel_multiplier=-1), [tri], [tri])
        P.do("pool", lambda e: e.memset(ones_f.t[:], 1.0), [], [ones_f])

        groups = [(g, tg) for g in heads for tg in range(8)]

        def prep(gi):
            g, tg = groups[gi]
            s = gi % 2
            c0 = tg * 512
            eb, qe, ke, kdtok, vtok = ebs[s], qes[s], kes[s], kdtoks[s], vtoks[s]
            gq, gk, gv, gg, glow = gqs.get(), gks.get(), gvs.get(), ggs.get(), glows.get()
            ctx[gi] = dict(gg=gg)

            def ld(buf, row0, nch):
                src = G.projT.t[row0:row0 + nch * 128, c0:c0 + 512].rearrange("(c p) t -> p c t", p=128)
                P.dma("sp", lambda e, buf=buf, src=src: e.dma_start(out=buf.t[:], in_=src), [], [buf], buf)
            ld(gq, GQ0 + g * 256, 2)
            ld(gk, GK0 + g * 256, 2)
            ld(gv, GV0 + g * 512, 4)
            ld(gg, GG0 + g * 512, 4)
            P.dma("sp", lambda e: e.dma_start(out=glow.t[:], in_=G.projF.t[16:32, c0:c0 + 512]), [], [glow], glow)
            yield
            for dh in range(2):
                P.mm([lambda e, dh=dh: e.matmul(ps_z[dh].t[:], wup.t[:, g * 256 + dh * 128:g * 256 + (dh + 1) * 128],
                                                 glow.t[:], start=True, stop=True)], [wup, glow], [ps_z[dh]])
            yield
            for dh in range(2):
                P.do("act", lambda e, dh=dh: e.activation(out=spb.t[:, dh, :], in_=ps_z[dh].t[:], func=AF.Exp,
                                                          bias=nbgk.t[:, g * 2 + dh:g * 2 + dh + 1], scale=-1.0),
                     [ps_z[dh], nbgk], [spb])
            yield
            P.do("act", lambda e: e.activation(out=spb.t[:], in_=spb.t[:], func=AF.Ln, bias=1.0), [spb], [spb])
            yield
            for dh in range(2):
                P.do("dve", lambda e, dh=dh: e.tensor_tensor_scan(out=bsp.t[:, dh, :], data0=rmask.t[:], data1=spb.t[:, dh, :],
                                                                 initial=0.0, op0=ALU.mult, op1=ALU.add), [spb, rmask], [bsp])
            yield
            P.do("act", lambda e: e.activation(out=eb.t[:], in_=bsp.t[:], func=AF.Exp, scale=-1.0 / 16), [bsp], [eb])
            P.do("act", lambda e: e.activation(out=enb.t[:], in_=bsp.t[:], func=AF.Exp, scale=1.0 / 16), [bsp], [enb])
            bv = bsp.t[:].rearrange("p d (c t) -> p (d c) t", t=128)
            dv = dd.t[:].rearrange("p d (c t) -> p (d c) t", t=128)
            P.do("dve", lambda e: e.tensor_tensor(out=dv, in0=bv[:, :, 127:128].broadcast_to([128, 8, 128]), in1=bv,
                                                  op=ALU.subtract), [bsp], [dd])
            yield
            P.do("act", lambda e: e.activation(out=dd.t[:], in_=dd.t[:], func=AF.Exp, scale=-1.0 / 16), [dd], [dd])
            P.do("dve", lambda e: e.scalar_tensor_tensor(out=qe.t[:], in0=gq.t[:], scalar=1.0 / 16, in1=eb.t[:],
                                                         op0=ALU.mult, op1=ALU.mult), [gq, eb], [qe])
            P.do("dve", lambda e: e.tensor_tensor(out=ke.t[:], in0=gk.t[:], in1=enb.t[:], op=ALU.mult), [gk, enb], [ke])
            yield
            P.do("dve", lambda e: e.tensor_tensor(out=kdT.t[:], in0=gk.t[:], in1=dd.t[:], op=ALU.mult), [gk, dd], [kdT])
            for half in range(2):
                P.mm([(lambda e, half=half, b=b, ec=ec: e.transpose(
                    ps_t.t[:, (b * 4 + ec) * 128:(b * 4 + ec + 1) * 128],
                    gv.t[:, ec, (half * 2 + b) * 128:(half * 2 + b + 1) * 128], G.ident.t[:]))
                    for b in range(2) for ec in range(4)], [gv, G.ident], [ps_t])
                yield
                P.do("act", lambda e, half=half: e.copy(out=vtok.t[:, half * 2:half * 2 + 2, :].rearrange("p b e -> p (b e)"),
                                                       in_=ps_t.t[:]), [ps_t], [vtok])
                yield
            P.mm([(lambda e, b=b, dh=dh: e.transpose(ps_t.t[:, (b * 2 + dh) * 128:(b * 2 + dh + 1) * 128],
                                                    kdT.t[:, dh, b * 128:(b + 1) * 128], G.ident.t[:]))
                  for b in range(4) for dh in range(2)], [kdT, G.ident], [ps_t])
            yield
            P.do("act", lambda e: e.copy(out=kdtok.t[:].rearrange("p b d -> p (b d)"), in_=ps_t.t[:]), [ps_t], [kdtok])

        def epilogue(gi):
            g, tg = groups[gi]
            s = gi % 2
            c0 = tg * 512
            og = ogs[s]
            gg = ctx[gi]["gg"]
            P.do("act", lambda e: e.activation(out=sq.t[:], in_=og.t[:], func=AF.Square), [og], [sq])
            yield
            P.mm([(lambda e, ec=ec: e.matmul(ps_ss.t[:], ones_f.t[:], sq.t[:, ec, :], start=(ec == 0), stop=(ec == 3)))
                  for ec in range(4)], [sq, ones_f], [ps_ss])
            yield
            P.do("act", lambda e: e.activation(out=rstd.t[:], in_=ps_ss.t[:], func=AF.Ln, bias=EPS, scale=1.0 / 512),
                 [ps_ss], [rstd])
            yield
            P.do("act", lambda e: e.activation(out=rstd.t[:], in_=rstd.t[:], func=AF.Exp, scale=-0.5), [rstd], [rstd])
            P.do("act", lambda e: e.activation(out=sig.t[:], in_=gg.t[:], func=AF.Exp, scale=-1.0), [gg], [sig])
            yield
            P.do("act", lambda e: e.activation(out=sig.t[:], in_=sig.t[:], func=AF.Ln, bias=1.0), [sig], [sig])
            yield
            P.do("act", lambda e: e.activation(out=sig.t[:], in_=sig.t[:], func=AF.Exp, scale=-1.0), [sig], [sig])
            yield
            P.do("dve", lambda e: e.tensor_tensor(out=sig.t[:], in0=sig.t[:], in1=gg.t[:], op=ALU.mult), [sig, gg], [sig])
            yield
            for ec in range(4):
                P.do("dve", lambda e, ec=ec: e.scalar_tensor_tensor(out=sq.t[:, ec, :], in0=og.t[:, ec, :],
                                                                   scalar=ggla.t[:, ec:ec + 1], in1=rstd.t[:],
                                                                   op0=ALU.mult, op1=ALU.mult), [og, rstd, ggla], [sq])
                if ec == 1:
                    yield
            yield
            P.do("dve", lambda e: e.tensor_tensor(out=ob.t[:], in0=sq.t[:], in1=sig.t[:], op=ALU.mult), [sq, sig], [ob])
            r0 = 2048 + g * 512
            dst = G.mixT.t[r0:r0 + 512, c0:c0 + 512].rearrange("(c p) t -> p c t", p=128)
            P.dma("sp", lambda e: e.dma_start(out=dst, in_=ob.t[:]), [ob], [], ob)

        ctx = {}
        side = []

        def pump(n=1):
            for _ in range(n):
                for gen in list(side):
                    try:
                        next(gen)
                    except StopIteration:
                        side.remove(gen)

        def drain(gen):
            for _ in gen:
                pass

        drain(prep(0))
        for gi, (g, tg) in enumerate(groups):
            s = gi % 2
            eb, qe, ke, kdtok, vtok, og = ebs[s], qes[s], kes[s], kdtoks[s], vtoks[s], ogs[s]
            if tg == 0:
                P.do("dve", lambda e: e.memset(Sst.t[:], 0.0), [], [Sst])
                P.do("pool", lambda e: e.memset(Sbf.t[:], 0.0), [], [Sbf])
            if gi + 1 < len(groups):
                side.append(prep(gi + 1))
            for b in range(4):
                ts = slice(b * 128, (b + 1) * 128)
                P.mm([(lambda e, dh=dh, ts=ts, ke=ke, qe=qe: e.matmul(ps_sc.t[:], ke.t[:, dh, ts], qe.t[:, dh, ts], start=(dh == 0),
                                                                     stop=(dh == 1))) for dh in range(2)], [ke, qe], [ps_sc])
                for dh in range(2):
                    P.mm([lambda e, dh=dh, b=b, kdtok=kdtok, vtok=vtok: e.matmul(
                        ps_su[dh].t[:], kdtok.t[:, b, dh * 128:(dh + 1) * 128], vtok.t[:, b, :], start=True, stop=True)],
                        [kdtok, vtok], [ps_su[dh]])
                scm = scms.get()
                P.do("dve", lambda e, scm=scm: e.tensor_tensor(out=scm.t[:], in0=ps_sc.t[:], in1=tri.t[:], op=ALU.mult),
                     [ps_sc, tri], [scm])
                pump()
                mms = []
                for ec in range(4):
                    es = slice(ec * 128, (ec + 1) * 128)
                    mms.append(lambda e, ec=ec, es=es, ts=ts, qe=qe: e.matmul(ps_o.t[:, ec, :], Sbf.t[:, 0, es], qe.t[:, 0, ts],
                                                                             start=True, stop=False))
                    mms.append(lambda e, ec=ec, es=es, ts=ts, qe=qe: e.matmul(ps_o.t[:, ec, :], Sbf.t[:, 1, es], qe.t[:, 1, ts],
                                                                             start=False, stop=False))
                    mms.append(lambda e, ec=ec, es=es, b=b, scm=scm, vtok=vtok: e.matmul(ps_o.t[:, ec, :], vtok.t[:, b, es], scm.t[:],
                                                                                        start=False, stop=True))
                P.mm(mms, [Sbf, qe, vtok, scm], [ps_o])
                pump()
                for dh in range(2):
                    col = b * 128 + 127
                    P.do("dve", lambda e, dh=dh, col=col, eb=eb: e.scalar_tensor_tensor(
                        out=Sst.t[:, dh, :], in0=Sst.t[:, dh, :], scalar=eb.t[:, dh, col:col + 1], in1=ps_su[dh].t[:],
                        op0=ALU.mult, op1=ALU.add), [Sst, eb, ps_su[dh]], [Sst])
                P.do("act", lambda e, ts=ts, og=og: e.copy(out=og.t[:, :, ts], in_=ps_o.t[:]), [ps_o], [og])
                P.do("act", lambda e: e.copy(out=Sbf.t[:], in_=Sst.t[:]), [Sst], [Sbf])
                pump(2)
            side.append(epilogue(gi))
        while side:
            pump()
        P.barrier()


def _layer_norm_tiles(P, st_, zs, gsrc, bsrc, gq, bq, junk, stats, ssum, ssq):
    for z, sc, s1, s2 in zip(zs, stats, ssum, ssq):
        P.do("dve", lambda e, sc=sc, s1=s1: e.reduce_sum(out=sc.t[:, 0:1], in_=s1.t[:], axis=mybir.AxisListType.X), [s1], [sc])
        P.do("dve", lambda e, sc=sc, s2=s2: e.reduce_sum(out=sc.t[:, 1:2], in_=s2.t[:], axis=mybir.AxisListType.X), [s2], [sc])
        P.do("dve", lambda e, sc=sc: e.tensor_scalar(out=sc.t[:, 2:4], in0=sc.t[:, 0:2], scalar1=1.0 / D, scalar2=None,
                                                   op0=ALU.mult), [sc], [sc])
        P.do("dve", lambda e, sc=sc: e.tensor_tensor(out=sc.t[:, 4:5], in0=sc.t[:, 2:3], in1=sc.t[:, 2:3], op=ALU.mult), [sc], [sc])
        P.do("dve", lambda e, sc=sc: e.tensor_tensor(out=sc.t[:, 5:6], in0=sc.t[:, 3:4], in1=sc.t[:, 4:5], op=ALU.subtract),
             [sc], [sc])
        P.do("act", lambda e, sc=sc: e.activation(out=sc.t[:, 6:7], in_=sc.t[:, 5:6], func=AF.Ln, bias=EPS), [sc], [sc])
        P.do("act", lambda e, sc=sc: e.activation(out=sc.t[:, 7:8], in_=sc.t[:, 6:7], func=AF.Exp, scale=-0.5), [sc], [sc])
    for qtr in range(4):
        cs = slice(qtr * 1024, (qtr + 1) * 1024)
        P.dma("sp", lambda e, cs=cs: e.dma_start(out=gq.t[:], in_=gsrc.t[cs].partition_broadcast(128)), [], [gq], gq)
        P.dma("sp", lambda e, cs=cs: e.dma_start(out=bq.t[:], in_=bsrc.t[cs].partition_broadcast(128)), [], [bq], bq)
        for z, sc in zip(zs, stats):
            P.do("dve", lambda e, z=z, sc=sc, cs=cs: e.scalar_tensor_tensor(out=z.t[:, cs], in0=z.t[:, cs], scalar=sc.t[:, 2:3],
                                                                          in1=gq.t[:], op0=ALU.subtract, op1=ALU.mult),
                 [z, sc, gq], [z])
            P.do("dve", lambda e, z=z, sc=sc, cs=cs: e.scalar_tensor_tensor(out=z.t[:, cs], in0=z.t[:, cs], scalar=sc.t[:, 7:8],
                                                                          in1=bq.t[:], op0=ALU.mult, op1=ALU.add),
                 [z, sc, bq], [z])


class WStream:
    def __init__(self, P, scr, pieces, npass, bufs, la):
        self.P, self.scr, self.pieces, self.bufs, self.la = P, scr, pieces, bufs, la
        self.total = npass * len(pieces)
        self.issued = 0
        self.used = 0
        self.stok = {}

    @staticmethod
    def view(w, kind, ncols):
        if kind == "f512":
            return w.t[:, 0:ncols].rearrange("p (f m) -> p f m", m=512)
        return w.t[:, 0:ncols].rearrange("p (k m) -> p k m", m=128)

    def _issue(self, p):
        P = self.P
        gidx, i = divmod(p, len(self.pieces))
        src, kind, ncols = self.pieces[i]
        w = self.bufs[p % len(self.bufs)]
        if gidx == 0:
            dst = self.view(w, kind, ncols)
            P.dma("pool", lambda e, dst=dst, src=src: e.dma_start(out=dst, in_=src), [], [w], w)
            if self.total > len(self.pieces):
                self.stok[i] = P.dma("sp", lambda e, w=w, i=i, ncols=ncols: e.dma_start(out=self.scr[i // 152].t[i % 152, :, 0:ncols],
                                                                                     in_=w.t[:, 0:ncols]), [w], [], w)
        else:
            ex = [self.stok[i]] if gidx == 1 else []
            P.dma("sp", lambda e, w=w, i=i, ncols=ncols: e.dma_start(out=w.t[:, 0:ncols], in_=self.scr[i // 152].t[i % 152, :, 0:ncols]),
                  [], [w], w, extra=ex)

    def next(self):
        p = self.used
        while self.issued < min(self.total, p + self.la + 1):
            self._issue(self.issued)
            self.issued += 1
        self.used += 1
        return self.bufs[p % len(self.bufs)]


def phase_C(P, G, l, x_res, x_out, groups=range(8)):
    QCH = [(0, 22), (22, 44), (44, 66), (66, 86)]
    with ExitStack() as st:
        zs = [P.sb([128, D], F32, st) for _ in range(4)]
        big = P.sb([128, 32, 512], BF16, st)
        h1T = P.sb([128, 32, 512], BF16, st)
        wp = RR([P.sb([128, 4096], BF16, st) for _ in range(6)])
        gq = P.sb([128, 1024], F32, st)
        bq = P.sb([128, 1024], F32, st)
        rbs = RR([P.sb([128, 512], F32, st) for _ in range(3)])
        sgs = RR([P.sb([128, 512], F32, st) for _ in range(2)])
        stat = [P.sb([128, 8], F32, st) for _ in range(4)]
        ssum = [P.sb([128, 8], F32, st) for _ in range(4)]
        ssq = [P.sb([128, 8], F32, st) for _ in range(4)]
        junkb = RR([P.sb([128, 512], BF16, st) for _ in range(2)])
        xb = P.sb([128, D], BF16, st)
        junk = xb
        pm = RR([P.ps([128, 512], F32, st) for _ in range(6)])
        pt = RR([P.ps([128, 1024], BF16, st) for _ in range(2)])
        pieces = []
        for n in range(8):
            for pc in range(4):
                pieces.append((G.wout[l].t[n, pc * 8:(pc + 1) * 8].rearrange("f p m -> p f m"), "f512", 4096))
        for (j0, j1) in QCH:
            for j in range(j0, j1):
                pieces.append((G.wg[l].t[j], "k128", 4096))
                pieces.append((G.wu[l].t[j], "k128", 4096))
            for n in range(8):
                ja = j0
                while ja < j1:
                    jb = min(ja + 8, j1)
                    pieces.append((G.wd[l].t[n, ja:jb].rearrange("f p m -> p f m"), "f512", (jb - ja) * 512))
                    ja = jb
        groups = list(groups)
        wst = WStream(P, G.wscr, pieces, len(groups), wp.b, len(wp.b) - 2)
        ev = 0
        h1tok = [None] * 4

        def block_stats(tt, n, ns):
            jb = junkb.get()
            P.do("act", lambda e, jb=jb, tt=tt, n=n, ns=ns: e.activation(out=jb.t[:], in_=zs[tt].t[:, ns], func=AF.Identity,
                                                                       accum_out=ssum[tt].t[:, n:n + 1]), [zs[tt]], [jb, ssum[tt]])
            P.do("act", lambda e, jb=jb, tt=tt, n=n, ns=ns: e.activation(out=jb.t[:], in_=zs[tt].t[:, ns], func=AF.Square,
                                                                       accum_out=ssq[tt].t[:, n:n + 1]), [zs[tt]], [jb, ssq[tt]])

        def zero_stats():
            for tt in range(4):
                P.do("pool", lambda e, tt=tt: e.memset(ssum[tt].t[:], 0.0), [], [ssum[tt]])
                P.do("pool", lambda e, tt=tt: e.memset(ssq[tt].t[:], 0.0), [], [ssq[tt]])
        for tg in groups:
            c0 = tg * 512
            zero_stats()
            for p4 in range(4):
                src = G.mixT.t[p4 * 1024:(p4 + 1) * 1024, c0:c0 + 512].rearrange("(c p) t -> p c t", p=128)
                P.dma("sp", lambda e, src=src, p4=p4: e.dma_start(out=big.t[:, p4 * 8:(p4 + 1) * 8, :], in_=src), [], [big], big)
            for n in range(8):
                ns = slice(n * 512, (n + 1) * 512)
                accs = [pm.get() for _ in range(4)]
                for half in range(2):
                    ws = []
                    for pc in (half * 2, half * 2 + 1):
                        w = wst.next()
                        ws.append((pc, w))
                    mms = []
                    for pc, w in ws:
                        wv = w.t[:].rearrange("p (f m) -> p f m", m=512)
                        for f8 in range(8):
                            fc = pc * 8 + f8
                            for tt in range(4):
                                mms.append(lambda e, wv=wv, f8=f8, fc=fc, tt=tt, acc=accs[tt]: e.matmul(
                                    acc.t[:], big.t[:, fc, tt * 128:(tt + 1) * 128], wv[:, f8, :], start=(fc == 0), stop=(fc == 31)))
                    P.mm(mms, [big] + [w for _, w in ws], accs)
                for tt in range(4):
                    rb = rbs.get()
                    r0 = c0 + tt * 128
                    P.dma("sp", lambda e, rb=rb, r0=r0, ns=ns: e.dma_start(out=rb.t[:], in_=x_res.t[r0:r0 + 128, ns]), [], [rb], rb)
                    P.do("dve", lambda e, tt=tt, rb=rb, ns=ns, acc=accs[tt]: e.scalar_tensor_tensor(
                        out=zs[tt].t[:, ns], in0=rb.t[:], scalar=ALPHA, in1=acc.t[:], op0=ALU.mult, op1=ALU.add),
                        [rb, accs[tt]], [zs[tt]])
                    block_stats(tt, n, ns)
            _layer_norm_tiles(P, st, zs, G.ln1g[l], G.ln1b[l], gq, bq, junk, stat, ssum, ssq)
            zero_stats()
            for tt in range(4):
                r0 = c0 + tt * 128
                h1tok[tt] = P.dma("sp", lambda e, tt=tt, r0=r0: e.dma_start(out=G.h1.t[r0:r0 + 128, :], in_=zs[tt].t[:]), [zs[tt]], [], zs[tt])
                P.do("act", lambda e, tt=tt: e.copy(out=xb.t[:], in_=zs[tt].t[:]), [zs[tt]], [xb])
                for k8 in range(4):
                    p = pt.get()
                    P.mm([(lambda e, p=p, k8=k8, j=j: e.transpose(p.t[:, j * 128:(j + 1) * 128],
                                                                 xb.t[:, (k8 * 8 + j) * 128:(k8 * 8 + j + 1) * 128], G.ident.t[:]))
                          for j in range(8)], [xb, G.ident], [p])
                    src = p.t[:].rearrange("p (j t) -> p j t", t=128)
                    dst = h1T.t[:, k8 * 8:(k8 + 1) * 8, tt * 128:(tt + 1) * 128]
                    if ev % 2 == 0:
                        P.do("dve", lambda e, src=src, dst=dst: e.tensor_copy(out=dst, in_=src), [p], [h1T])
                    else:
                        P.do("act", lambda e, src=src, dst=dst: e.copy(out=dst, in_=src), [p], [h1T])
                    ev += 1
            for qi, (j0, j1) in enumerate(QCH):
                for j in range(j0, j1):
                    wg, wu = wst.next(), wst.next()
                    pg, pu = pm.get(), pm.get()
                    wgv = wg.t[:].rearrange("p (k m) -> p k m", m=128)
                    wuv = wu.t[:].rearrange("p (k m) -> p k m", m=128)
                    P.mm([(lambda e, kc=kc, wgv=wgv, pg=pg: e.matmul(pg.t[:], wgv[:, kc, :], h1T.t[:, kc, :], start=(kc == 0),
                                                                    stop=(kc == 31))) for kc in range(32)], [wg, h1T], [pg])
                    P.mm([(lambda e, kc=kc, wuv=wuv, pu=pu: e.matmul(pu.t[:], wuv[:, kc, :], h1T.t[:, kc, :], start=(kc == 0),
                                                                    stop=(kc == 31))) for kc in range(32)], [wu, h1T], [pu])
                    sg = sgs.get()
                    P.do("act", lambda e, sg=sg, pg=pg: e.activation(out=sg.t[:], in_=pg.t[:], func=AF.Silu), [pg], [sg])
                    P.do("dve", lambda e, sg=sg, pu=pu, jj=j - j0: e.tensor_tensor(out=big.t[:, jj, :], in0=pu.t[:], in1=sg.t[:],
                                                                                 op=ALU.mult), [pu, sg], [big])
                for n in range(8):
                    ns = slice(n * 512, (n + 1) * 512)
                    accs = [pm.get() for _ in range(4)]
                    ja = j0
                    while ja < j1:
                        jb = min(ja + 8, j1)
                        w = wst.next()
                        wv = w.t[:].rearrange("p (f m) -> p f m", m=512)
                        mms = []
                        for j in range(ja, jb):
                            for tt in range(4):
                                mms.append(lambda e, wv=wv, j=j, ja=ja, tt=tt, acc=accs[tt], j0=j0, j1=j1: e.matmul(
                                    acc.t[:], big.t[:, j - j0, tt * 128:(tt + 1) * 128], wv[:, j - ja, :], start=(j == j0),
                                    stop=(j == j1 - 1)))
                        P.mm(mms, [big, w], accs)
                        ja = jb
                    for tt in range(4):
                        if qi == 0:
                            rb = rbs.get()
                            r0 = c0 + tt * 128
                            P.dma("sp", lambda e, rb=rb, r0=r0, ns=ns: e.dma_start(out=rb.t[:], in_=G.h1.t[r0:r0 + 128, ns]),
                                  [], [rb], rb, extra=[h1tok[tt]])
                            P.do("dve", lambda e, tt=tt, rb=rb, ns=ns, acc=accs[tt]: e.scalar_tensor_tensor(
                                out=zs[tt].t[:, ns], in0=rb.t[:], scalar=ALPHA, in1=acc.t[:], op0=ALU.mult, op1=ALU.add),
                                [rb, accs[tt]], [zs[tt]])
                        else:
                            P.do("dve", lambda e, tt=tt, ns=ns, acc=accs[tt]: e.tensor_tensor(
                                out=zs[tt].t[:, ns], in0=acc.t[:], in1=zs[tt].t[:, ns], op=ALU.add), [accs[tt], zs[tt]], [zs[tt]])
                            if qi == len(QCH) - 1:
                                block_stats(tt, n, ns)
            _layer_norm_tiles(P, st, zs, G.ln2g[l], G.ln2b[l], gq, bq, junk, stat, ssum, ssq)
            for tt in range(4):
                r0 = c0 + tt * 128
                P.dma("sp", lambda e, tt=tt, r0=r0: e.dma_start(out=x_out.t[r0:r0 + 128, :], in_=zs[tt].t[:]), [zs[tt]], [], zs[tt])
        P.barrier()


class Globals:
    pass


def build_program(mode="full"):
    nc = bass.Bass("TRN2", target_bir_lowering=False)
    P = Prog(nc)
    G = Globals()
    G.inputs = {}

    def inp(name, shape, dt=F32):
        G.inputs[name] = tuple(shape)
        return P.dram(name, shape, dt, "ExternalInput")

    layers = [0] if mode in ("A0", "AB0", "C0", "L0") else [0, 1]
    need_A = mode in ("A0", "AB0", "L0", "full")
    need_B = mode in ("AB0", "L0", "full")
    need_C = mode in ("C0", "L0", "full")
    G.x = inp("x", [S, D]) if need_A else None
    if need_A:
        G.ln_in_g = inp("ln_in_g", [D])
        G.ln_in_b = inp("ln_in_b", [D])
    G.win, G.wsm, G.b_f, G.w_gk_up, G.bgk, G.gfox, G.ggla = {}, {}, {}, {}, {}, {}, {}
    G.wout, G.ln1g, G.ln1b, G.wg, G.wu, G.wd, G.ln2g, G.ln2b = {}, {}, {}, {}, {}, {}, {}, {}
    for l in layers:
        if need_A:
            G.win[l] = inp(f"win{l}", [96, 128, 32, 128])
            G.wsm[l] = inp(f"wsm{l}", [128, 32, 32])
        if need_B:
            G.b_f[l] = inp(f"b_f{l}", [16, 1])
            G.w_gk_up[l] = inp(f"wgkup{l}", [16, 1024])
            G.bgk[l] = inp(f"bgk{l}", [128, 8])
            G.gfox[l] = inp(f"gfox{l}", [128, 16])
            G.ggla[l] = inp(f"ggla{l}", [128, 4])
        if need_C:
            G.wout[l] = inp(f"wout{l}", [8, 32, 128, 512])
            G.ln1g[l] = inp(f"ln1g{l}", [D])
            G.ln1b[l] = inp(f"ln1b{l}", [D])
            G.wg[l] = inp(f"wg{l}", [NFF, 128, 32, 128])
            G.wu[l] = inp(f"wu{l}", [NFF, 128, 32, 128])
            G.wd[l] = inp(f"wd{l}", [8, NFF, 128, 512])
            G.ln2g[l] = inp(f"ln2g{l}", [D])
            G.ln2b[l] = inp(f"ln2b{l}", [D])
    if mode == "A0":
        G.projT = P.dram("projT", [12288, S], BF16, "ExternalOutput")
        G.projF = P.dram("projF", [32, S], F32, "ExternalOutput")
        G.xn = P.dram("xn", [S, D], F32, "ExternalOutput")
    else:
        G.projT = P.dram("projT", [12288, S], BF16)
        G.projF = P.dram("projF", [32, S], F32)
        G.xn = P.dram("xn", [S, D], F32) if mode != "C0" else inp("xn", [S, D])
    if mode == "AB0":
        G.mixT = P.dram("mixT", [D, S], BF16, "ExternalOutput")
    elif mode == "C0":
        G.mixT = inp("mixT", [D, S], BF16)
    else:
        G.mixT = P.dram("mixT", [D, S], BF16)
    G.h1 = P.dram("h1", [S, D], F32)
    G.wscr = [P.dram(f"wscr{k}", [152, 128, 4096], BF16) for k in range(2)]
    G.x1 = P.dram("x1", [S, D], F32) if mode == "full" else None
    if mode in ("C0", "L0", "full"):
        G.out = P.dram("out", [S, D], F32, "ExternalOutput")
    G.ident = P.sb([128, 128], BF16)
    P.do("pool", lambda e: e.memset(G.ident.t[:], 1.0), [], [G.ident])
    P.do("pool", lambda e: e.affine_select(out=G.ident.t[:], in_=G.ident.t[:], pattern=[[-1, 128]], compare_op=ALU.is_equal,
                                           fill=0.0, base=0, channel_multiplier=1), [G.ident], [G.ident])
    for l in layers:
        last = (l == layers[-1])
        if need_A:
            phase_A(P, G, l, G.x if l == 0 else G.x1, apply_ln=(l == 0))
        if need_B:
            phase_B_fox(P, G, l)
            phase_B_gla(P, G, l)
        if need_C:
            phase_C(P, G, l, G.xn if l == 0 else G.x1, G.out if last else G.x1)
    P.barrier()
    P.finish()
    return nc, G


def host_layouts(inputs, layers=(0, 1), need=("A", "B", "C")):
    m = {}
    if "A" in need:
        m["ln_in_g"] = np.ascontiguousarray(inputs["ln_in_g"], dtype=np.float32)
        m["ln_in_b"] = np.ascontiguousarray(inputs["ln_in_b"], dtype=np.float32)
    for l in layers:
        if "A" in need:
            w = inputs["w_in"][l]
            big = np.concatenate([w[:, 0:6144], w[:, 6160:12304]], axis=1)
            m[f"win{l}"] = np.ascontiguousarray(big.reshape(32, 128, 96, 128).transpose(2, 1, 0, 3))
            small = np.concatenate([w[:, 6144:6160], w[:, 12304:12320]], axis=1)
            m[f"wsm{l}"] = np.ascontiguousarray(small.reshape(32, 128, 32).transpose(1, 0, 2))
        if "B" in need:
            m[f"b_f{l}"] = np.ascontiguousarray(inputs["b_f"][l].reshape(16, 1))
            m[f"wgkup{l}"] = np.ascontiguousarray(inputs["w_gk_up"][l])
            m[f"bgk{l}"] = np.ascontiguousarray(inputs["b_gk"][l].reshape(8, 128).T)
            m[f"gfox{l}"] = np.ascontiguousarray(inputs["fox_norm_g"][l].T)
            m[f"ggla{l}"] = np.ascontiguousarray(inputs["gla_norm_g"][l].reshape(4, 128).T)
        if "C" in need:
            m[f"wout{l}"] = np.ascontiguousarray(inputs["w_out"][l].reshape(32, 128, 8, 512).transpose(2, 0, 1, 3))
            m[f"ln1g{l}"] = np.ascontiguousarray(inputs["ln1_g"][l])
            m[f"ln1b{l}"] = np.ascontiguousarray(inputs["ln1_b"][l])
            m[f"wg{l}"] = np.ascontiguousarray(inputs["w_gate"][l].reshape(32, 128, NFF, 128).transpose(2, 1, 0, 3))
            m[f"wu{l}"] = np.ascontiguousarray(inputs["w_up"][l].reshape(32, 128, NFF, 128).transpose(2, 1, 0, 3))
            m[f"wd{l}"] = np.ascontiguousarray(inputs["w_down"][l].reshape(NFF, 128, 8, 512).transpose(2, 0, 1, 3))
            m[f"ln2g{l}"] = np.ascontiguousarray(inputs["ln2_g"][l])
            m[f"ln2b{l}"] = np.ascontiguousarray(inputs["ln2_b"][l])
    return m


def kernel(**inputs):
    inputs = {k: np.asarray(v) for k, v in inputs.items()}
    nc, G = build_program("full")
    shared = host_layouts(inputs)
    x = np.asarray(inputs["x"], dtype=np.float32)
    in_maps = []
    for b in range(2):
        m = dict(shared)
        m["x"] = np.ascontiguousarray(x[b])
        in_maps.append(m)
    res = run_bass_kernel_spmd(nc, in_maps, core_ids=[0, 1])
    out = np.stack([np.asarray(res.results[b]["out"]) for b in range(2)], axis=0)
    return out.astype(np.float32)
```
